# Optimizing a Trainium2 kernel written in Bass

```python
import jax, jax.numpy as jnp
from jax import lax
import numpy as np

D_MODEL = 1024
BATCH = 1
SEQ = 16384
DEPTH = 1

MIX_W = D_MODEL
HEAD_DIM = 64
ATTN_W = MIX_W // 2
N_ATTN_HEADS = ATTN_W // HEAD_DIM
N_KV_HEADS = 2
KV_W = N_KV_HEADS * HEAD_DIM
WINDOW = 128
BLOCK = 128
RET_W = MIX_W - ATTN_W
N_RET_HEADS = 4
RET_HEAD_DIM = RET_W // N_RET_HEADS
RET_CHUNK = 128
IN_W = ATTN_W + 2 * KV_W + 4 * RET_W
D_FF = 2816
CONV_WIDTH = 3
RMS_EPS = 1e-6
GN_EPS = 1e-6
MASK_VALUE = -1e30

kernel_name = "hymba_swa_sink_retention_convffn_sandwich"


def rms_norm(x, w):
    xf = x.astype(jnp.float32)
    y = xf * lax.rsqrt(jnp.mean(xf * xf, axis=-1, keepdims=True) + RMS_EPS)
    return (y * w.astype(jnp.float32)).astype(x.dtype)


def sliding_window_sink_attention(q, k, v, sinks):
    b, s, _ = q.shape
    nb = s // BLOCK
    g = N_ATTN_HEADS // N_KV_HEADS
    qb = q.reshape(b, nb, BLOCK, N_KV_HEADS, g, HEAD_DIM)
    kb = k.reshape(b, nb, BLOCK, N_KV_HEADS, HEAD_DIM)
    vb = v.reshape(b, nb, BLOCK, N_KV_HEADS, HEAD_DIM)
    pad = ((0, 0), (1, 0), (0, 0), (0, 0), (0, 0))
    kk = jnp.concatenate([jnp.pad(kb, pad)[:, :-1], kb], axis=2)
    vv = jnp.concatenate([jnp.pad(vb, pad)[:, :-1], vb], axis=2)
    scores = jnp.einsum('bnqhgd,bnkhd->bnhgqk', qb, kk).astype(jnp.float32) * (HEAD_DIM ** -0.5)
    qpos = jnp.arange(BLOCK)[:, None] + BLOCK
    kpos = jnp.arange(2 * BLOCK)[None, :]
    rel = qpos - kpos
    band = (rel >= 0) & (rel < WINDOW)
    not_pad = (jnp.arange(nb)[:, None, None] > 0) | (kpos >= BLOCK)[None]
    valid = band[None] & not_pad
    scores = jnp.where(valid[None, :, None, None], scores, MASK_VALUE)
    sink = jnp.broadcast_to(
        sinks.astype(jnp.float32).reshape(N_KV_HEADS, g)[None, None, :, :, None, None],
        scores.shape[:-1] + (1,))
    probs = jax.nn.softmax(jnp.concatenate([scores, sink], axis=-1), axis=-1)[..., :-1]
    out = jnp.einsum('bnhgqk,bnkhd->bnqhgd', probs.astype(vv.dtype), vv)
    return out.reshape(b, s, ATTN_W)


def rotate_every_two(x):
    x1 = x[..., ::2]
    x2 = x[..., 1::2]
    return jnp.stack([-x2, x1], axis=-1).reshape(x.shape)


def retention_chunkwise(q, k, v):
    b, s, h, dk = q.shape
    dv = v.shape[-1]
    nc = s // RET_CHUNK
    c = RET_CHUNK
    log_gamma = jnp.log(1.0 - jnp.power(2.0, -5.0 - jnp.arange(h, dtype=jnp.float32)))
    idx = jnp.arange(c, dtype=jnp.float32)
    rel = idx[:, None] - idx[None, :]
    d_intra = jnp.where(rel[None] >= 0,
                        jnp.exp(log_gamma[:, None, None] * jnp.maximum(rel, 0.0)[None]), 0.0)
    xi = jnp.exp(log_gamma[None, :] * (idx[:, None] + 1.0))
    zeta = jnp.exp(log_gamma[None, :] * (c - 1.0 - idx[:, None]))
    chunk_decay = jnp.exp(log_gamma * c)
    qc = q.reshape(b, nc, c, h, dk)
    kc = k.reshape(b, nc, c, h, dk)
    vc = v.reshape(b, nc, c, h, dv)
    inner = jnp.einsum('bnqhd,bnkhd->bnhqk', qc, kc) * d_intra[None, None]
    o_inner = jnp.einsum('bnhqk,bnkhe->bnqhe', inner, vc)
    kv_chunk = jnp.einsum('bnkhd,kh,bnkhe->bnhde', kc, zeta, vc)

    def step(state, kv):
        return chunk_decay[None, :, None, None] * state + kv, state

    _, prev = lax.scan(step, jnp.zeros((b, h, dk, dv), jnp.float32), jnp.moveaxis(kv_chunk, 1, 0))
    prev = jnp.moveaxis(prev, 0, 1)
    o_cross = jnp.einsum('bnqhd,bnhde->bnqhe', qc, prev) * xi[None, None, :, :, None]
    return (o_inner + o_cross).reshape(b, s, h, dv)


def retention_group(q, k, v, gate):
    b, s, _ = q.shape
    dtype = q.dtype
    pos = jnp.arange(s, dtype=jnp.float32)
    angle = 1.0 / jnp.power(10000.0, jnp.linspace(0.0, 1.0, RET_HEAD_DIM // 2, dtype=jnp.float32))
    angle = jnp.repeat(angle, 2)
    sin = jnp.sin(pos[:, None] * angle[None])[None, :, None, :]
    cos = jnp.cos(pos[:, None] * angle[None])[None, :, None, :]
    qf = q.astype(jnp.float32).reshape(b, s, N_RET_HEADS, RET_HEAD_DIM)
    kf = k.astype(jnp.float32).reshape(b, s, N_RET_HEADS, RET_HEAD_DIM) * (RET_HEAD_DIM ** -0.5)
    vf = v.astype(jnp.float32).reshape(b, s, N_RET_HEADS, RET_HEAD_DIM)
    qf = qf * cos + rotate_every_two(qf) * sin
    kf = kf * cos + rotate_every_two(kf) * sin
    o = retention_chunkwise(qf, kf, vf)
    mu = jnp.mean(o, axis=-1, keepdims=True)
    var = jnp.mean(jnp.square(o - mu), axis=-1, keepdims=True)
    o = ((o - mu) * lax.rsqrt(var + GN_EPS)).reshape(b, s, RET_W)
    return (jax.nn.silu(gate.astype(jnp.float32)) * o).astype(dtype)


def causal_depthwise_conv(u, w, bias):
    ch = u.shape[-1]
    y = lax.conv_general_dilated(u, w[:, None, :].astype(u.dtype), window_strides=(1,),
                                 padding=[(CONV_WIDTH - 1, 0)],
                                 dimension_numbers=('NWC', 'WIO', 'NWC'),
                                 feature_group_count=ch)
    return y + bias.astype(u.dtype)


def setup_inputs(seed: int = 0) -> dict:
    key = jax.random.key(seed)
    ks = jax.random.split(key, 13)
    f32 = jnp.float32

    def gain(k):
        return 1.0 + 0.02 * jax.random.normal(k, (DEPTH, D_MODEL), f32)

    return {
        "x": jax.random.normal(ks[0], (BATCH, SEQ, D_MODEL), f32),
        "mix_pre_norm": gain(ks[1]),
        "w_in": jax.random.normal(ks[2], (DEPTH, D_MODEL, IN_W), f32) * D_MODEL ** -0.5,
        "attn_sinks": jax.random.normal(ks[3], (DEPTH, N_ATTN_HEADS), f32),
        "w_out": jax.random.normal(ks[4], (DEPTH, MIX_W, D_MODEL), f32) * MIX_W ** -0.5,
        "mix_post_norm": gain(ks[5]),
        "ffn_pre_norm": gain(ks[6]),
        "w_up": jax.random.normal(ks[7], (DEPTH, D_MODEL, 2 * D_FF), f32) * D_MODEL ** -0.5,
        "conv_w": jax.random.normal(ks[8], (DEPTH, CONV_WIDTH, 2 * D_FF), f32) * CONV_WIDTH ** -0.5,
        "conv_b": 0.01 * jax.random.normal(ks[9], (DEPTH, 2 * D_FF), f32),
        "w_down": jax.random.normal(ks[10], (DEPTH, D_FF, D_MODEL), f32) * D_FF ** -0.5,
        "ffn_post_norm": gain(ks[11]),
    }


def reference(x, mix_pre_norm, w_in, attn_sinks, w_out, mix_post_norm,
              ffn_pre_norm, w_up, conv_w, conv_b, w_down, ffn_post_norm):
    splits = np.cumsum([ATTN_W, KV_W, KV_W, RET_W, RET_W, RET_W]).tolist()
    for l in range(DEPTH):
        h = rms_norm(x, mix_pre_norm[l])
        proj = jnp.einsum('bsd,de->bse', h, w_in[l])
        q_a, k_a, v_a, q_r, k_r, v_r, g_r = jnp.split(proj, splits, axis=-1)
        attn_out = sliding_window_sink_attention(q_a, k_a, v_a, attn_sinks[l])
        ret_out = retention_group(q_r, k_r, v_r, g_r)
        mixed = jnp.einsum('bse,ed->bsd', jnp.concatenate([attn_out, ret_out], axis=-1), w_out[l])
        x = x + rms_norm(mixed, mix_post_norm[l])
        h = rms_norm(x, ffn_pre_norm[l])
        u = causal_depthwise_conv(jnp.einsum('bsd,df->bsf', h, w_up[l]), conv_w[l], conv_b[l])
        u_gate, u_val = jnp.split(u, 2, axis=-1)
        y = jax.nn.gelu(u_gate, approximate=True) * u_val
        y = jnp.einsum('bsf,fd->bsd', y, w_down[l])
        x = x + rms_norm(y, ffn_post_norm[l])
    return x
```

```python
import math
from contextlib import ExitStack

import numpy as np
import concourse.bass as bass
import concourse.mybir as mybir
from concourse.bass_utils import run_bass_kernel_spmd

F32 = mybir.dt.float32
BF16 = mybir.dt.bfloat16
AF = mybir.ActivationFunctionType
ALU = mybir.AluOpType

NCORES = 8
S = 16384
D = 1024
TPC = S // NCORES
NCH = TPC // 128
IN_W = 2816
DFF = 2816
NFB = DFF // 128
EPS = 1e-6
GAMMA = [1.0 - 2.0 ** (-5.0 - h) for h in range(4)]
CDECAY = [g ** 128 for g in GAMMA]
NPRE = 44
NX = NPRE + NCH + 1
NM = NCH + 1


class Buf:
    __slots__ = ("name", "w", "r")

    def __init__(self, name):
        self.name = name
        self.w = None
        self.r = []


class DSem:
    def __init__(self, h):
        self.h = h
        self.count = 0


class Op:
    __slots__ = ("eng", "emit", "raw", "oth", "sig", "dsem", "dval", "needed", "dinc")

    def __init__(self):
        self.dsem = None


class Prog:
    ENG = ("sp", "act", "dve", "pool", "pe")

    def __init__(self):
        self.ops = {e: [] for e in self.ENG}
        self.dma_since_barrier = []
        self.stopped = False

    def op(self, eng, emit, reads=(), writes=(), dsem=None, ndma=0, extra=(), dinc=16):
        o = Op()
        if self.stopped:
            return o
        o.eng, o.emit, o.sig, o.needed = eng, emit, None, False
        o.dsem = dsem
        o.dval = None
        o.dinc = dinc
        if dsem is not None:
            dsem.count += dinc * ndma
            o.dval = dsem.count
            self.dma_since_barrier.append(o)
        raw, oth = [], []
        for b in reads:
            if b.w is not None:
                raw.append(b.w)
        for b in writes:
            if b.w is not None:
                oth.append(b.w)
            oth.extend(b.r)
        oth.extend(extra)
        o.raw = [d for d in dict.fromkeys(raw) if d is not o]
        o.oth = [d for d in dict.fromkeys(oth) if d is not o and d not in o.raw]
        for b in reads:
            b.r.append(o)
        for b in writes:
            b.w = o
            b.r = []
        self.ops[eng].append(o)
        return o

    def barrier(self):
        if self.stopped:
            return
        lasts = [self.ops[e][-1] for e in self.ENG if self.ops[e]]
        dmas = list(self.dma_since_barrier)
        self.dma_since_barrier = []
        for e in self.ENG:
            self.op(e, None, extra=[x for x in lasts + dmas])

    def finalize(self):
        for e in self.ENG:
            for o in self.ops[e]:
                for d in o.raw:
                    if d.dsem is None and (d.eng != o.eng or o.eng != "pe"):
                        d.needed = True
                for d in o.oth:
                    if d.dsem is None and d.eng != o.eng:
                        d.needed = True
        for e in self.ENG:
            n = 0
            for o in self.ops[e]:
                if o.needed:
                    n += 1
                    o.sig = n

    def emit_engine(self, e, engobj, esems):
        waited = {}

        def wait(sem, val):
            k = id(sem)
            if waited.get(k, 0) >= val:
                return
            waited[k] = val
            engobj.wait_ge(sem, val)

        for o in self.ops[e]:
            for d, israw in [(d, True) for d in o.raw] + [(d, False) for d in o.oth]:
                if d.dsem is not None:
                    wait(d.dsem.h, d.dval)
                elif d.eng == e:
                    if e != "pe" and israw:
                        wait(esems[e], d.sig)
                else:
                    wait(esems[d.eng], d.sig)
            if o.emit is None:
                continue
            r = o.emit(engobj)
            if o.dsem is not None:
                for ins in r:
                    if o.dinc == 1:
                        ins.then_inc(o.dsem.h)
                    else:
                        ins.then_inc(o.dsem.h, o.dinc)
            elif o.needed:
                r.then_inc(esems[e], 1)


def build_program(stop=99):
    nc = bass.Bass("TRN2", target_bir_lowering=False)
    P = Prog()

    def stage(k):
        if stop < k:
            P.stopped = True

    def din(name, shape):
        return nc.dram_tensor(name, shape, F32, kind="ExternalInput").ap()

    x_ext = din("x_ext", [NX * 128, D])
    w_in = din("w_in", [D, IN_W])
    w_out = din("w_out", [D, D])
    w_up = din("w_up", [D, 2 * DFF])
    w_down = din("w_down", [DFF, D])
    gains = din("gains", [4, D])
    sinks = din("sinks", [1, 8])
    conv_w = din("conv_w", [128, 3 * 44])
    conv_b = din("conv_b", [128, 44])
    ident = din("ident", [128, 128])
    amask_d = din("amask", [128, 3, 512])
    rmask_d = din("rmask", [128, 512])
    tq = din("tq", [NM * 128, 2, 512])
    tk = din("tk", [NX * 128, 2, 512])
    coef = din("coef", [1, 40])
    out = nc.dram_tensor("out", [TPC, D], F32, kind="ExternalOutput").ap()
    x1d = nc.dram_tensor("x1d", [TPC, D], F32)

    es = ExitStack()
    with es:
        def sb(name, shape, dt):
            return es.enter_context(nc.sbuf_tensor(name, shape, dt))

        def sem(name):
            return es.enter_context(nc.semaphore(name))

        def dsem(name):
            return DSem(sem(name))

        esems = {e: sem("e_" + e) for e in Prog.ENG}

        WA = 38912
        warena = sb("warena", [128, WA], BF16)
        aarena = sb("aarena", [128, 17408], BF16)
        FA = 14336
        farena = sb("farena", [128, FA], F32)

        def fa(off, n):
            return farena[:, off:off + n]

        identb = sb("identb", [128, 128], BF16)
        wkd = sb("wkd", [128, 8, 2, 128], BF16)
        hT = [sb("hT%d" % i, [128, 8, 128], BF16) for i in range(2)]
        xns = [sb("xn%d" % i, [128, 1024], BF16) for i in range(2)]
        xn = xns[0]
        qaT = sb("qaT", [128, 2, 512], BF16)
        kT = [sb("kT%d" % i, [128, 2, 128], BF16) for i in range(2)]
        vaug = [sb("vaug%d" % i, [128, 2, 128], BF16) for i in range(2)]
        ones64 = sb("ones64", [128, 128], BF16)
        PT = [sb("PT%d" % i, [128, 512], BF16) for i in range(4)]
        PTm = [sb("PTm%d" % i, [128, 512], BF16) for i in range(4)]
        kztmp = [PT[0], PT[1]]
        vrtmp = [PT[2], PT[3]]
        attnT = sb("attnT", [128, 2, 2, 128], BF16)
        qx = sb("qx", [128, 512], BF16)
        qxT = sb("qxT", [128, 4, 128], BF16)
        kzT = sb("kzT", [128, 4, 128], BF16)
        inT = sb("inT", [128, 512], BF16)
        stbf = sb("stbf", [128, 512], BF16)
        gated = sb("gated", [128, 512], BF16)
        retT = sb("retT", [128, 4, 128], BF16)
        amask = sb("amaskb", [128, 3, 512], BF16)
        ssq = sb("ssq", [128, 192], F32)
        xh = sb("xh", [2, 1024], F32)
        stat = sb("stat", [128, 128], F32)
        h2Th = sb("h2Th", [128, 8, 2], BF16)

        pb0 = es.enter_context(nc.psum_tensor("pb0", [128, 1024], BF16))
        pb1 = es.enter_context(nc.psum_tensor("pb1", [128, 1024], BF16))
        pf = [None, None] + [es.enter_context(nc.psum_tensor("pf%d" % i, [128, 512], F32)) for i in range(2, 8)]
        B_pb0, B_pb1 = Buf("pb0"), Buf("pb1")
        B_pf = [None, None] + [Buf("pf%d" % i) for i in range(2, 8)]

        winv = warena[:, 0:22528].rearrange("p (k n) -> p k n", k=8)
        woa = warena[:, 22528:26624].rearrange("p (h n) -> p h n", h=4)
        wor = warena[:, 26624:30720].rearrange("p (h n) -> p h n", h=4)
        wdn = warena[:, 0:22528].rearrange("p (j n) -> p j n", j=NFB)
        wupr = [warena[:, 22528 + i * 4096:22528 + (i + 1) * 4096].rearrange("p (k n) -> p k n", k=8) for i in range(4)]
        kzs = aarena[:, 0:8704].rearrange("p (c n) -> p c n", c=NM)
        vrs = aarena[:, 8704:17408].rearrange("p (c n) -> p c n", c=NM)
        yT = aarena[:, 0:11264].rearrange("p (j n) -> p j n", j=NFB)
        h2T = aarena[:, 11264:15360].rearrange("p (k n) -> p k n", k=8)

        B = {}

        def bf(name):
            if name not in B:
                B[name] = Buf(name)
            return B[name]

        ssq_col = [0]

        def newcol():
            ssq_col[0] += 1
            assert ssq_col[0] < 192
            return ssq_col[0] - 1

        GH = [0, 2, 1, 3]
        w_in_v = w_in.rearrange("(k p) n -> p k n", p=128)

        s_const = dsem("s_const")
        s_w = [dsem("s_w%d" % i) for i in range(6)]

        g_pre1, g_post1 = fa(0, 1024), fa(1024, 1024)
        xc = [fa(2048, 1024), fa(3072, 1024)]
        tab = [fa(4096, 1024).rearrange("p (a n) -> p a n", a=2), fa(5120, 1024).rearrange("p (a n) -> p a n", a=2)]
        Gst = fa(2048, 4096).rearrange("p (r n) -> p r n", r=8)
        t1, t2, sg, eg = fa(6144, 512), fa(6656, 512), fa(7168, 512), fa(7680, 512)
        x1o = [fa(8192, 1024), fa(9216, 1024)]
        esk = fa(10240, 1024).rearrange("p (j n) -> p j n", j=2)
        rmask = fa(11264, 512)
        state = fa(11776, 512)
        rn = fa(12288, 512)
        den = fa(12800, 512)
        tmpm = fa(13312, 1024)
        coefb = stat[:, 64:104]
        es8 = stat[:, 40:48]

        def c_loads(e):
            r = []
            r.append(e.dma_start(out=g_pre1, in_=gains[0:1, :].partition_broadcast(128)[:, 0, :]))
            r.append(e.dma_start(out=g_post1, in_=gains[1:2, :].partition_broadcast(128)[:, 0, :]))
            r.append(e.dma_start(out=rmask, in_=rmask_d[:, :]))
            r.append(e.dma_start(out=coefb, in_=coef.partition_broadcast(128)[:, 0, :]))
            r.append(e.dma_start(out=es8, in_=sinks.partition_broadcast(128)[:, 0, :]))
            return r
        P.op("sp", c_loads, writes=[bf("g_pre1"), bf("g_post1"), bf("rmask"), bf("coefb"), bf("es8")], dsem=s_const, ndma=5)

        def c_loads2(e):
            r = []
            r.append(e.dma_start(out=identb[:], in_=ident[:, :]))
            r.append(e.dma_start(out=amask[:], in_=amask_d[:, :, :]))
            return r
        s_const2 = dsem("s_const2")
        P.op("pool", c_loads2, writes=[bf("identb"), bf("amask")], dsem=s_const2, ndma=2)

        def w_a0(e):
            return [e.dma_start(out=winv[:, :, 1280:2304], in_=w_in_v[:, :, 1280:2304])]
        P.op("pool", w_a0, writes=[bf("win_kv")], dsem=s_w[0], ndma=1)

        def memsets(e):
            e.memset(ssq[:], 0.0)
            e.memset(state, 0.0)
            e.memset(qaT[:], 0.0)
            return e.memset(ones64[:], 1.0)
        P.op("dve", memsets, writes=[bf("ssq"), bf("state"), bf("ones64"), bf("qaT0")])

        def w_a1(e):
            r = [e.dma_start(out=winv[:, :, 0:1280], in_=w_in_v[:, :, 0:1280]),
                 e.dma_start(out=winv[:, :, 2304:2816], in_=w_in_v[:, :, 2304:2816])]
            for j in range(2):
                for dup in range(2):
                    r.append(e.dma_start(out=wkd[:, :, j, dup * 64:(dup + 1) * 64],
                                         in_=w_in_v[:, :, 512 + 64 * j:512 + 64 * (j + 1)]))
            for j in range(2):
                for a in range(2):
                    hl, hu = 4 * j + GH[2 * a], 4 * j + GH[2 * a + 1]
                    r.append(e.dma_start(out=woa[0:64, 2 * j + a, :], in_=w_out[64 * hl:64 * hl + 64, :]))
                    r.append(e.dma_start(out=woa[64:128, 2 * j + a, :], in_=w_out[64 * hu:64 * hu + 64, :]))
            r.append(e.dma_start(out=wor, in_=w_out[512:1024, :].rearrange("(h p) n -> p h n", p=128)))
            return r

        xsem = [dsem("xs0"), dsem("xs1")]
        tsem = [dsem("ts0"), dsem("ts1")]

        def norm_transpose(xrow0, slot, gain_ap, gain_buf, src_dram, load=True, xbuf=None, xap=None, dst=None, dstbuf=None, lsem=None, pbk=0):
            xa = xap if xap is not None else xc[slot]
            xb_ = xbuf if xbuf is not None else bf("xc%d" % slot)
            xs = slot % 2
            xn_ = xns[xs]
            xnb = bf("xn%d" % xs)
            cl, cr = (48, 49) if xs == 0 else (4, 5)
            lnb, rsb = bf("lnv%d" % xs), bf("rstd%d" % xs)
            pbt, pbb = (pb0, B_pb0) if pbk == 0 else (pb1, B_pb1)
            if load:
                P.op("sp", lambda e: [e.dma_start(out=xa, in_=src_dram[xrow0:xrow0 + 128, :])],
                     writes=[xb_], dsem=(lsem if lsem is not None else xsem[slot]), ndma=1)
            col = newcol()
            P.op("act", lambda e: e.activation(out=xn_[:], in_=xa, func=AF.Square, accum_out=ssq[:, col:col + 1]),
                 reads=[xb_, bf("ssq")], writes=[xnb, bf("ssqc%d" % xs)])
            P.op("act", lambda e: e.activation(out=stat[:, cl:cl + 1], in_=ssq[:, col:col + 1], func=AF.Ln, scale=1.0 / D, bias=stat[:, 63:64]),
                 reads=[bf("ssqc%d" % xs), bf("epsb")], writes=[lnb])
            P.op("act", lambda e: e.activation(out=stat[:, cr:cr + 1], in_=stat[:, cl:cl + 1], func=AF.Exp, scale=-0.5),
                 reads=[lnb], writes=[rsb])
            P.op("dve", lambda e: e.scalar_tensor_tensor(out=xn_[:], in0=xa, scalar=stat[:, cr:cr + 1], in1=gain_ap, op0=ALU.mult, op1=ALU.mult),
                 reads=[xb_, rsb, gain_buf], writes=[xnb])

            def tr(e):
                r = None
                for k in range(8):
                    r = e.transpose(out=pbt[:, k * 128:(k + 1) * 128], in_=xn_[:, k * 128:(k + 1) * 128], identity=identb[:])
                return r
            P.op("pe", tr, reads=[xnb, bf("identb")], writes=[pbb])
            d_ap = dst if dst is not None else hT[slot][:]
            d_buf = dstbuf if dstbuf is not None else bf("hT%d" % slot)
            P.op("act", lambda e: e.activation(out=d_ap, in_=pbt[:].rearrange("p (k n) -> p k n", k=8), func=AF.Copy),
                 reads=[pbb], writes=[d_buf])

        def cst(e):
            e.memset(stat[:, 62:63], 1.0)
            return e.memset(stat[:, 63:64], EPS)
        P.op("pool", cst, writes=[bf("epsb"), bf("oneb")])

        t1s, t2s = [t1, rn], [t2, den]
        t1b, t2b = [bf("t1"), bf("rn")], [bf("t2"), bf("den")]

        def rotate_ops(psbuf, psap, tslot, out_ap, out_buf, ts=0):
            tb = bf("tab%d" % tslot)
            t1_, t2_ = t1s[ts], t2s[ts]
            P.op("dve", lambda e: e.tensor_tensor(out=t1_, in0=psap, in1=tab[tslot][:, 0, :], op=ALU.mult),
                 reads=[psbuf, tb], writes=[t1b[ts]])

            def sw(e):
                pv = psap.rearrange("p (i two) -> p i two", two=2)
                sv = tab[tslot][:, 1, :].rearrange("p (i two) -> p i two", two=2)
                ov = t2_.rearrange("p (i two) -> p i two", two=2)
                e.tensor_tensor(out=ov[:, :, 0], in0=pv[:, :, 1], in1=sv[:, :, 0], op=ALU.mult)
                return e.tensor_tensor(out=ov[:, :, 1], in0=pv[:, :, 0], in1=sv[:, :, 1], op=ALU.mult)
            P.op("dve", sw, reads=[psbuf, tb], writes=[t2b[ts]])
            P.op("dve", lambda e: e.tensor_tensor(out=out_ap, in0=t1_, in1=t2_, op=ALU.add),
                 reads=[t1b[ts], t2b[ts]], writes=[out_buf])

        def kv_state_update(kap, vap, kbuf, vbuf, kb_=4):
            def kvmm(e):
                r = None
                for h in range(4):
                    r = e.matmul(pf[kb_][:, h * 128:(h + 1) * 128], lhsT=kap[:, h * 128:(h + 1) * 128],
                                 rhs=vap[:, h * 128:(h + 1) * 128], start=True, stop=True)
                return r
            P.op("pe", kvmm, reads=[kbuf, vbuf], writes=[B_pf[kb_]])

            def upd(e):
                r = None
                for h in range(4):
                    sl = slice(h * 128, (h + 1) * 128)
                    r = e.scalar_tensor_tensor(out=state[:, sl], in0=state[:, sl], scalar=float(CDECAY[h]), in1=pf[kb_][:, sl],
                                               op0=ALU.mult, op1=ALU.add)
                return r
            P.op("dve", upd, reads=[B_pf[kb_], bf("state")], writes=[bf("state")])

        stage(1)
        def lnrs(xs):
            return ((48, 49) if xs == 0 else (4, 5)), bf("lnv%d" % xs), bf("rstd%d" % xs)

        def pre_s1(i):
            slot = i % 2
            xa, xb_, xn_, xnb = xc[slot], bf("xc%d" % slot), xns[slot], bf("xn%d" % slot)
            (cl, cr), lnb, rsb = lnrs(slot)
            P.op("sp", lambda e: [e.dma_start(out=xa, in_=x_ext[128 * i:128 * i + 128, :])], writes=[xb_], dsem=xsem[slot], ndma=1)
            if i == 0:
                P.op("pool", w_a1, writes=[bf("win_rest"), bf("wkd"), bf("wout")], dsem=s_w[1], ndma=15)
            col = newcol()
            P.op("act", lambda e: e.activation(out=xn_[:], in_=xa, func=AF.Square, accum_out=ssq[:, col:col + 1]),
                 reads=[xb_, bf("ssq")], writes=[xnb, bf("ssqc%d" % slot)])
            P.op("act", lambda e: e.activation(out=stat[:, cl:cl + 1], in_=ssq[:, col:col + 1], func=AF.Ln, scale=1.0 / D, bias=stat[:, 63:64]),
                 reads=[bf("ssqc%d" % slot), bf("epsb")], writes=[lnb])
            P.op("act", lambda e: e.activation(out=stat[:, cr:cr + 1], in_=stat[:, cl:cl + 1], func=AF.Exp, scale=-0.5),
                 reads=[lnb], writes=[rsb])
            P.op("dve", lambda e: e.scalar_tensor_tensor(out=xn_[:], in0=xa, scalar=stat[:, cr:cr + 1], in1=g_pre1, op0=ALU.mult, op1=ALU.mult),
                 reads=[xb_, rsb, bf("g_pre1")], writes=[xnb])

        def pre_s2(i):
            slot = i % 2
            xn_, xnb = xns[slot], bf("xn%d" % slot)
            pbt, pbb = (pb0, B_pb0) if slot == 0 else (pb1, B_pb1)

            def tr(e):
                r = None
                for k in range(8):
                    r = e.transpose(out=pbt[:, k * 128:(k + 1) * 128], in_=xn_[:, k * 128:(k + 1) * 128], identity=identb[:])
                return r
            P.op("pe", tr, reads=[xnb, bf("identb")], writes=[pbb])
            P.op("act", lambda e: e.activation(out=hT[slot][:], in_=pbt[:].rearrange("p (k n) -> p k n", k=8), func=AF.Copy),
                 reads=[pbb], writes=[bf("hT%d" % slot)])

        def pre_kv(i):
            slot = i % 2
            if i < NPRE:
                return kztmp[slot][:], vrtmp[slot][:], bf("kztmp%d" % slot), bf("vrtmp%d" % slot)
            m = i - NPRE
            return kzs[:, m, :], vrs[:, m, :], bf("kz%d" % m), bf("vr%d" % m)

        def pre_s3(i):
            slot = i % 2
            bk, bv = (3, 5) if slot == 0 else (6, 7)

            def proj(e):
                r = None
                for (bank, c0) in ((bk, 1280), (bv, 1792)):
                    for k in range(8):
                        r = e.matmul(pf[bank][:], lhsT=hT[slot][:, k, :], rhs=winv[:, k, c0:c0 + 512], start=(k == 0), stop=(k == 7))
                return r
            P.op("pe", proj, reads=[bf("hT%d" % slot), bf("win_kv")], writes=[B_pf[bk], B_pf[bv]])
            kap, vap, kb, vb = pre_kv(i)
            P.op("act", lambda e: e.activation(out=vap, in_=pf[bv][:], func=AF.Copy), reads=[B_pf[bv]], writes=[vb])
            rotate_ops(B_pf[bk], pf[bk][:], slot, kap, kb, ts=slot)

        def pre_s4(i):
            if i >= NPRE:
                return
            kap, vap, kb, vb = pre_kv(i)
            kv_state_update(kap, vap, kb, vb, kb_=(4 if i % 2 == 0 else 2))

        for j in range(-2, NX + 1):
            if 0 <= j + 2 < NX:
                pre_s1(j + 2)
            if 0 <= j + 1 < NX:
                pre_s2(j + 1)
            if 0 <= j < NX:
                pre_s3(j)
            if 0 <= j - 1 < NX:
                pre_s4(j - 1)
            if 0 <= j + 2 < NX:
                P.op("pool", lambda e, i=j + 2: [e.dma_start(out=tab[i % 2], in_=tk[i * 128:(i + 1) * 128, :, :])],
                     writes=[bf("tab%d" % ((j + 2) % 2))], dsem=tsem[(j + 2) % 2], ndma=1)

        stage(2)
        stage(3)
        P.op("act", lambda e: e.activation(out=es8, in_=es8, func=AF.Exp), reads=[bf("es8")], writes=[bf("es8")])

        def esk_fill(e):
            r = None
            for j in range(2):
                for gs in range(4):
                    hq = 4 * j + GH[gs]
                    r = e.activation(out=esk[:, j, gs * 128:(gs + 1) * 128], in_=rmask[:, 0:128], func=AF.Identity, scale=0.0, bias=es8[:, hq:hq + 1])
            return r
        P.op("act", esk_fill, reads=[bf("es8"), bf("rmask")], writes=[bf("esk")])

        def kv_from_hT(slot_h, slot_kv, load_v=True):
            def mm(e):
                r = None
                for j in range(2):
                    for k in range(8):
                        r = e.matmul(pf[3][:, j * 128:(j + 1) * 128], lhsT=wkd[:, k, j, :], rhs=hT[slot_h][:, k, :], start=(k == 0), stop=(k == 7))
                for k in range(8):
                    r = e.matmul(pf[3][:, 256:384], lhsT=hT[slot_h][:, k, :], rhs=winv[:, k, 640:768], start=(k == 0), stop=(k == 7))
                return r
            P.op("pe", mm, reads=[bf("hT%d" % slot_h), bf("wkd"), bf("win_rest")], writes=[B_pf[3]])
            P.op("act", lambda e: e.activation(out=kT[slot_kv][:], in_=pf[3][:, 0:256].rearrange("p (j n) -> p j n", j=2), func=AF.Copy),
                 reads=[B_pf[3]], writes=[bf("kT%d" % slot_kv)])
            def vevac(e):
                e.activation(out=vaug[slot_kv][:, :, 0:64], in_=pf[3][:, 256:384].rearrange("p (j n) -> p j n", j=2), func=AF.Copy)
                return e.activation(out=vaug[slot_kv][:, :, 64:128], in_=pf[3][:, 256:384].rearrange("p (j n) -> p j n", j=2), func=AF.Copy)
            P.op("act", vevac, reads=[B_pf[3]], writes=[bf("va%d" % slot_kv)])

        norm_transpose(128 * (NPRE - 1), 0, g_pre1, bf("g_pre1"), x_ext)
        kv_from_hT(0, 1)

        def main_head(m):
            slot = m % 2
            P.op("pool", lambda e, m=m, slot=slot: [e.dma_start(out=tab[slot], in_=tq[m * 128:(m + 1) * 128, :, :])],
                 writes=[bf("tab%d" % slot)], dsem=tsem[slot], ndma=1)
            norm_transpose(128 * (NPRE + m), slot, g_pre1, bf("g_pre1"), x_ext)

        x1sem = [dsem("x1s0"), dsem("x1s1")]
        s_hb = dsem("s_hb")
        for m in range(NM):
            n = m - 1
            if m >= 1:
                stage(4 + m * 0.01)
            slot = m % 2
            cur, prv = m % 2, (m + 1) % 2
            if m == 0:
                main_head(0)

            def proj_q(e, slot=slot):
                r = None
                for pr in range(4):
                    for k in range(8):
                        r = e.matmul(pf[2][:, pr * 128:(pr + 1) * 128], lhsT=winv[:, k, pr * 128:(pr + 1) * 128], rhs=hT[slot][:, k, :],
                                     start=(k == 0), stop=(k == 7))
                return r
            P.op("pe", proj_q, reads=[bf("hT%d" % slot), bf("win_rest")], writes=[B_pf[2]])
            def qevac(e):
                e.activation(out=qaT[0:64, :, 0:256], in_=pf[2][0:64, :].rearrange("p (j n) -> p j n", j=2), func=AF.Copy)
                return e.activation(out=qaT[64:128, :, 256:512], in_=pf[2][64:128, :].rearrange("p (j n) -> p j n", j=2), func=AF.Copy)
            P.op("act", qevac, reads=[B_pf[2], bf("qaT0")], writes=[bf("qaT")])
            kv_from_hT(slot, cur)

            def proj_r(e, slot=slot):
                r = None
                for (bank, c0) in ((4, 768), (5, 2304)):
                    for k in range(8):
                        r = e.matmul(pf[bank][:], lhsT=hT[slot][:, k, :], rhs=winv[:, k, c0:c0 + 512], start=(k == 0), stop=(k == 7))
                return r
            P.op("pe", proj_r, reads=[bf("hT%d" % slot), bf("win_rest")], writes=[B_pf[4], B_pf[5]])
            rotate_ops(B_pf[4], pf[4][:], slot, qx[:], bf("qx"))
            P.op("act", lambda e: e.activation(out=eg, in_=pf[5][:], func=AF.Exp, scale=-1.0), reads=[B_pf[5]], writes=[bf("eg")])
            P.op("act", lambda e: e.activation(out=eg, in_=eg, func=AF.Ln, scale=1.0, bias=stat[:, 62:63]), reads=[bf("eg"), bf("oneb")], writes=[bf("eg")])
            P.op("act", lambda e: e.activation(out=eg, in_=eg, func=AF.Exp, scale=-1.0), reads=[bf("eg")], writes=[bf("eg")])
            P.op("dve", lambda e: e.tensor_tensor(out=sg, in0=eg, in1=pf[5][:], op=ALU.mult), reads=[bf("eg"), B_pf[5]], writes=[bf("sg")])

            stage(3.2 if m == 0 else 4 + m * 0.01 + 0.002)
            if m + 1 < NM:
                main_head(m + 1)
            first = (m == 1)
            for j in range(2):
                for bi, (kslot, mi) in enumerate(((prv, 2 if first else 1), (cur, 0))):
                    ti = j * 2 + bi
                    bank = 6 + (ti % 2)

                    def sc(e, j=j, kslot=kslot, bank=bank):
                        return e.matmul(pf[bank][:], lhsT=kT[kslot][:, j, :], rhs=qaT[:, j, :], start=True, stop=True)
                    P.op("pe", sc, reads=[bf("kT%d" % kslot), bf("qaT")], writes=[B_pf[bank]])
                    P.op("act", lambda e, ti=ti, bank=bank: e.activation(out=PT[ti][:], in_=pf[bank][:], func=AF.Exp, scale=0.125),
                         reads=[B_pf[bank]], writes=[bf("PT%d" % ti)])
                    P.op("dve", lambda e, ti=ti, mi=mi: e.tensor_tensor(out=PTm[ti][:], in0=PT[ti][:], in1=amask[:, mi, :], op=ALU.mult),
                         reads=[bf("PT%d" % ti), bf("amask")], writes=[bf("PTm%d" % ti)])
            for j in range(2):
                def pv(e, j=j, prv=prv, cur=cur):
                    e.matmul(pf[2][:], lhsT=vaug[prv][:, j, :], rhs=PTm[2 * j][:], start=True, stop=False)
                    e.matmul(pf[2][:], lhsT=vaug[cur][:, j, :], rhs=PTm[2 * j + 1][:], start=False, stop=True)
                    e.matmul(pf[3][:], lhsT=ones64[:], rhs=PTm[2 * j][:], start=True, stop=False)
                    return e.matmul(pf[3][:], lhsT=ones64[:], rhs=PTm[2 * j + 1][:], start=False, stop=True)
                P.op("pe", pv, reads=[bf("va%d" % prv), bf("va%d" % cur), bf("PTm%d" % (2 * j)), bf("PTm%d" % (2 * j + 1)), bf("ones64")],
                     writes=[B_pf[2], B_pf[3]])
                def lnden(e, j=j):
                    r = None
                    for gs in range(4):
                        hq = 4 * j + GH[gs]
                        r = e.activation(out=den[:, gs * 128:(gs + 1) * 128], in_=pf[3][:, gs * 128:(gs + 1) * 128], func=AF.Ln,
                                         scale=1.0, bias=es8[:, hq:hq + 1])
                    return r
                P.op("act", lnden, reads=[B_pf[3], bf("es8")], writes=[bf("den")])
                P.op("act", lambda e: e.activation(out=den, in_=den, func=AF.Exp, scale=-1.0), reads=[bf("den")], writes=[bf("den")])

                def anorm(e, j=j):
                    ov = pf[2][:].rearrange("p (a b q) -> p a b q", a=2, b=2)
                    dv = den.rearrange("p (a b q) -> p a b q", a=2, b=2)
                    e.tensor_tensor(out=attnT[0:64, j, :, :], in0=ov[0:64, :, 0, :], in1=dv[0:64, :, 0, :], op=ALU.mult)
                    return e.tensor_tensor(out=attnT[64:128, j, :, :], in0=ov[64:128, :, 1, :], in1=dv[64:128, :, 1, :], op=ALU.mult)
                P.op("dve", anorm, reads=[B_pf[2], bf("den")], writes=[bf("attnT%d" % j)])

            stage(3.4 if m == 0 else 4 + m * 0.01 + 0.004)
            def trqk(e, n=m):
                r = None
                for h in range(4):
                    r = e.transpose(out=pb1[:, h * 128:(h + 1) * 128], in_=qx[:, h * 128:(h + 1) * 128], identity=identb[:])
                for h in range(4):
                    r = e.transpose(out=pb1[:, 512 + h * 128:512 + (h + 1) * 128], in_=kzs[:, n, h * 128:(h + 1) * 128], identity=identb[:])
                return r
            P.op("pe", trqk, reads=[bf("qx"), bf("kz%d" % m), bf("identb")], writes=[B_pb1])
            P.op("act", lambda e: e.activation(out=qxT[:], in_=pb1[:, 0:512].rearrange("p (h n) -> p h n", h=4), func=AF.Copy),
                 reads=[B_pb1], writes=[bf("qxT")])
            P.op("act", lambda e: e.activation(out=kzT[:], in_=pb1[:, 512:1024].rearrange("p (h n) -> p h n", h=4), func=AF.Copy),
                 reads=[B_pb1], writes=[bf("kzT")])

            def inner(e):
                r = None
                for h in range(4):
                    r = e.matmul(pf[4][:, h * 128:(h + 1) * 128], lhsT=kzT[:, h, :], rhs=qxT[:, h, :], start=True, stop=True)
                return r
            P.op("pe", inner, reads=[bf("kzT"), bf("qxT")], writes=[B_pf[4]])
            P.op("dve", lambda e: e.tensor_tensor(out=inT[:], in0=pf[4][:], in1=rmask, op=ALU.mult),
                 reads=[B_pf[4], bf("rmask")], writes=[bf("inT")])
            P.op("pool", lambda e: e.tensor_copy(out=stbf[:], in_=state), reads=[bf("state")], writes=[bf("stbf")])

            def omm(e, n=m):
                r = None
                for h in range(4):
                    sl = slice(h * 128, (h + 1) * 128)
                    e.matmul(pf[6][:, sl], lhsT=inT[:, sl], rhs=vrs[:, n, sl], start=True, stop=False)
                    r = e.matmul(pf[6][:, sl], lhsT=qxT[:, h, :], rhs=stbf[:, sl], start=False, stop=True)
                return r
            P.op("pe", omm, reads=[bf("inT"), bf("vr%d" % m), bf("qxT"), bf("stbf")], writes=[B_pf[6]])
            if m < NM - 1:
                kv_state_update(kzs[:, m, :], vrs[:, m, :], bf("kz%d" % m), bf("vr%d" % m))

            def gn_stats(e):
                r = None
                for h in range(4):
                    r = e.bn_stats(out=stat[:, 8 + 6 * h:14 + 6 * h], in_=pf[6][:, h * 128:(h + 1) * 128])
                return r
            P.op("dve", gn_stats, reads=[B_pf[6]], writes=[bf("bnst")])

            def gn_aggr(e):
                r = None
                for h in range(4):
                    r = e.bn_aggr(out=stat[:, 32 + 2 * h:34 + 2 * h], in_=stat[:, 8 + 6 * h:14 + 6 * h])
                return r
            P.op("dve", gn_aggr, reads=[bf("bnst")], writes=[bf("mv")])
            mvv = stat[:, 32:40].rearrange("p (h two) -> p h two", two=2)
            P.op("act", lambda e: e.activation(out=stat[:, 0:4], in_=mvv[:, :, 1], func=AF.Ln, scale=1.0, bias=stat[:, 63:64]),
                 reads=[bf("mv"), bf("epsb")], writes=[bf("glnv")])
            P.op("act", lambda e: e.activation(out=stat[:, 50:54], in_=stat[:, 0:4], func=AF.Exp, scale=-0.5),
                 reads=[bf("glnv")], writes=[bf("grstd")])
            P.op("dve", lambda e: e.scalar_tensor_tensor(out=stat[:, 54:58], in0=mvv[:, :, 0], scalar=-1.0, in1=stat[:, 50:54], op0=ALU.mult, op1=ALU.mult),
                 reads=[bf("mv"), bf("grstd")], writes=[bf("gnmr")])

            def gn_apply(e):
                r = None
                for h in range(4):
                    sl = slice(h * 128, (h + 1) * 128)
                    r = e.activation(out=rn[:, sl], in_=pf[6][:, sl], func=AF.Identity, scale=stat[:, 50 + h:51 + h], bias=stat[:, 54 + h:55 + h])
                return r
            P.op("act", gn_apply, reads=[B_pf[6], bf("grstd"), bf("gnmr")], writes=[bf("rn")])
            P.op("dve", lambda e: e.tensor_tensor(out=gated[:], in0=rn, in1=sg, op=ALU.mult), reads=[bf("rn"), bf("sg")], writes=[bf("gated")])

            def trr(e):
                r = None
                for h in range(4):
                    r = e.transpose(out=pb1[:, h * 128:(h + 1) * 128], in_=gated[:, h * 128:(h + 1) * 128], identity=identb[:])
                return r
            P.op("pe", trr, reads=[bf("gated"), bf("identb")], writes=[B_pb1])
            P.op("act", lambda e: e.activation(out=retT[:], in_=pb1[:, 0:512].rearrange("p (h n) -> p h n", h=4), func=AF.Copy),
                 reads=[B_pb1], writes=[bf("retT")])

            stage(3.6 if m == 0 else 4 + m * 0.01 + 0.006)
            def wout(e):
                r = None
                for cb in range(2):
                    bank = 6 + cb
                    cs = slice(cb * 512, (cb + 1) * 512)
                    first_mm = True
                    for j in range(2):
                        for a in range(2):
                            e.matmul(pf[bank][:], lhsT=attnT[:, j, a, :], rhs=woa[:, 2 * j + a, cs], start=first_mm, stop=False)
                            first_mm = False
                    for h in range(4):
                        r = e.matmul(pf[bank][:], lhsT=retT[:, h, :], rhs=wor[:, h, cs], start=False, stop=(h == 3))
                return r
            P.op("pe", wout, reads=[bf("attnT0"), bf("attnT1"), bf("retT"), bf("wout")], writes=[B_pf[6], B_pf[7]])

            stage(3.8 if m == 0 else 4 + m * 0.01 + 0.008)
            c0, c1 = newcol(), newcol()

            def sq2(e, c0=c0, c1=c1):
                e.activation(out=PT[0][:], in_=pf[6][:], func=AF.Square, accum_out=ssq[:, c0:c0 + 1])
                return e.activation(out=PT[1][:], in_=pf[7][:], func=AF.Square, accum_out=ssq[:, c1:c1 + 1])
            P.op("act", sq2, reads=[B_pf[6], B_pf[7], bf("ssq")], writes=[bf("PT0"), bf("PT1"), bf("ssqm")])
            P.op("dve", lambda e, c0=c0, c1=c1: e.tensor_tensor(out=stat[:, 58:59], in0=ssq[:, c0:c0 + 1], in1=ssq[:, c1:c1 + 1], op=ALU.add),
                 reads=[bf("ssqm")], writes=[bf("ssqs")])
            P.op("act", lambda e: e.activation(out=stat[:, 59:60], in_=stat[:, 58:59], func=AF.Ln, scale=1.0 / D, bias=stat[:, 63:64]),
                 reads=[bf("ssqs"), bf("epsb")], writes=[bf("lnv2")])
            P.op("act", lambda e: e.activation(out=stat[:, 60:61], in_=stat[:, 59:60], func=AF.Exp, scale=-0.5),
                 reads=[bf("lnv2")], writes=[bf("rstd2")])

            def post(e):
                e.scalar_tensor_tensor(out=tmpm[:, 0:512], in0=pf[6][:], scalar=stat[:, 60:61], in1=g_post1[:, 0:512], op0=ALU.mult, op1=ALU.mult)
                return e.scalar_tensor_tensor(out=tmpm[:, 512:1024], in0=pf[7][:], scalar=stat[:, 60:61], in1=g_post1[:, 512:1024], op0=ALU.mult, op1=ALU.mult)
            P.op("dve", post, reads=[B_pf[6], B_pf[7], bf("rstd2"), bf("g_post1")], writes=[bf("tmpm")])
            oslot = m % 2
            P.op("dve", lambda e, slot=slot, oslot=oslot: e.tensor_tensor(out=x1o[oslot], in0=tmpm, in1=xc[slot], op=ALU.add),
                 reads=[bf("tmpm"), bf("xc%d" % slot)], writes=[bf("x1o%d" % oslot)])
            if m == 0:
                P.op("sp", lambda e, oslot=oslot: [e.dma_start(out=xh[:], in_=x1o[oslot][126:128, :])],
                     reads=[bf("x1o%d" % oslot)], writes=[bf("xh")], dsem=s_hb, ndma=1)
            else:
                P.op("sp", lambda e, n=n, oslot=oslot: [e.dma_start(out=x1d[n * 128:(n + 1) * 128, :], in_=x1o[oslot])],
                     reads=[bf("x1o%d" % oslot)], writes=[bf("x1d%d" % n)], dsem=x1sem[oslot], ndma=1)

        stage(5)
        P.barrier()
        stage(6)

        g_pre2, g_post2 = fa(0, 1024), fa(1024, 1024)
        xbt = fa(2048, 4096).rearrange("p (c n) -> p c n", c=4)
        U = [fa(6144 + i * 516, 516) for i in range(4)]
        acc = [fa(8208 + i * 512, 512) for i in range(4)]
        gl = [fa(10256 + i * 512, 512) for i in range(2)]
        ob = [fa(11280 + i * 1024, 1024) for i in range(2)]
        carry = fa(13328, 88).rearrange("p (j t) -> p j t", t=2)
        cw = fa(13416, 132).rearrange("p (k j) -> p k j", k=3)
        cbv = fa(13548, 44)

        s_b = dsem("s_b")

        def b_loads(e):
            r = []
            r.append(e.dma_start(out=g_pre2, in_=gains[2:3, :].partition_broadcast(128)[:, 0, :]))
            r.append(e.dma_start(out=g_post2, in_=gains[3:4, :].partition_broadcast(128)[:, 0, :]))
            r.append(e.dma_start(out=fa(13416, 132), in_=conv_w[:, :]))
            r.append(e.dma_start(out=cbv, in_=conv_b[:, :]))
            return r
        P.op("sp", b_loads, writes=[bf("g_pre2"), bf("g_post2"), bf("cw"), bf("cbv")], dsem=s_b, ndma=4)
        s_wd = dsem("s_wd")
        w_down_v = w_down.rearrange("(j p) n -> p j n", p=128)
        w_up_v = w_up.rearrange("(k p) n -> p k n", p=128)

        NPC = 6
        pw = [512] * 5 + [256]
        ring_sem = [dsem("s_up%d" % i) for i in range(4)]
        ring_state = {"i": 0}

        def load_piece(part, q):
            i = ring_state["i"] % 4
            ring_state["i"] += 1
            c0 = part * DFF + q * 512
            w = pw[q]
            P.op("pool", lambda e, i=i, c0=c0, w=w: [e.dma_start(out=wupr[i][:, :, 0:w], in_=w_up_v[:, :, c0:c0 + w])],
                 writes=[bf("wupr%d" % i)], dsem=ring_sem[i], ndma=1)
            return i

        seq = [(q, part) for q in range(NPC) for part in (0, 1)]

        colh = newcol()
        P.op("act", lambda e: e.activation(out=xns[1][0:2, :], in_=xh[:], func=AF.Square, accum_out=ssq[0:2, colh:colh + 1]),
             reads=[bf("xh"), bf("ssq")], writes=[bf("xn1"), bf("ssqc")])
        P.op("act", lambda e: e.activation(out=stat[0:2, 48:49], in_=ssq[0:2, colh:colh + 1], func=AF.Ln, scale=1.0 / D, bias=stat[0:2, 63:64]),
             reads=[bf("ssqc"), bf("epsb")], writes=[bf("lnv")])
        P.op("act", lambda e: e.activation(out=stat[0:2, 49:50], in_=stat[0:2, 48:49], func=AF.Exp, scale=-0.5),
             reads=[bf("lnv")], writes=[bf("rstd")])
        P.op("dve", lambda e: e.memset(xn[:], 0.0), writes=[bf("xn0")])
        P.op("dve", lambda e: e.scalar_tensor_tensor(out=xn[0:2, :], in0=xh[:], scalar=stat[0:2, 49:50], in1=g_pre2[0:2, :], op0=ALU.mult, op1=ALU.mult),
             reads=[bf("xh"), bf("rstd"), bf("g_pre2"), bf("xn0")], writes=[bf("xn0")])

        def trh(e):
            r = None
            for k in range(8):
                r = e.transpose(out=pb0[:, k * 128:(k + 1) * 128], in_=xn[:, k * 128:(k + 1) * 128], identity=identb[:])
            return r
        P.op("pe", trh, reads=[bf("xn0"), bf("identb")], writes=[B_pb0])
        P.op("act", lambda e: e.activation(out=h2Th[:], in_=pb0[:].rearrange("p (k n) -> p k n", k=8)[:, :, 0:2], func=AF.Copy),
             reads=[B_pb0], writes=[bf("h2Th")])

        xbsem = [dsem("xb%d" % i) for i in range(4)]

        def x1_load(t, c):
            n = 4 * t + c
            q = "sp" if c % 2 == 0 else "pool"
            P.op(q, lambda e: [e.dma_start(out=xbt[:, c, :], in_=x1d[n * 128:(n + 1) * 128, :])],
                 writes=[bf("xbt%d" % c)], dsem=xbsem[c], ndma=1)

        def x1_norm(t, c):
            n = 4 * t + c
            norm_transpose(128 * n, c % 2, g_pre2, bf("g_pre2"), x1d.ap(), load=False, xbuf=bf("xbt%d" % c), xap=xbt[:, c, :],
                           dst=h2T[:, :, c * 128:(c + 1) * 128], dstbuf=bf("h2T%d" % c), lsem=xbsem[c], pbk=c % 2)
        osem = [dsem("os0"), dsem("os1")]
        out_ops = []
        NT = NCH // 4
        for t in range(NT):
            if t == 1:
                stage(7)
            if t == 0:
                for c in range(4):
                    x1_load(0, c)
                for c in range(4):
                    x1_norm(0, c)
            if t == 0:
                P.op("pool", lambda e: [e.dma_start(out=wdn[:, a:b, :], in_=w_down_v[:, a:b, :]) for (a, b) in ((0, 6), (6, 12), (12, 18), (18, 22))],
                     writes=[bf("wdn")], dsem=s_wd, ndma=4)
            pieces = {}
            pending = list(seq)
            for _ in range(3):
                q, part = pending.pop(0)
                pieces[(q, part)] = load_piece(part, q)
            ui = 0
            deferred = []
            for q in range(NPC):
                nb = pw[q] // 128
                for _ in range(1 if q == 0 else 2):
                    if pending:
                        q2, p2 = pending.pop(0)
                        pieces[(q2, p2)] = load_piece(p2, q2)
                for jb in range(nb):
                    j = q * 4 + jb
                    accs = []
                    taps = []
                    for part in (0, 1):
                        ri = pieces[(q, part)]
                        jj = part * NFB + j
                        bank = 2 + (ui % 4)
                        us = ui % 4
                        ui += 1
                        wsl = slice(jb * 128, (jb + 1) * 128)
                        if t == 0:
                            def cmm(e, ri=ri, wsl=wsl):
                                r = None
                                for k in range(8):
                                    r = e.matmul(pf[6][:, 0:2], lhsT=wupr[ri][:, k, wsl], rhs=h2Th[:, k, :], start=(k == 0), stop=(k == 7))
                                return r
                            P.op("pe", cmm, reads=[bf("wupr%d" % ri), bf("h2Th")], writes=[B_pf[6]])
                            P.op("act", lambda e, jj=jj: e.activation(out=carry[:, jj, :], in_=pf[6][:, 0:2], func=AF.Copy),
                                 reads=[B_pf[6]], writes=[bf("carry%d" % jj)])

                        def umm(e, ri=ri, wsl=wsl, bank=bank):
                            r = None
                            for k in range(8):
                                r = e.matmul(pf[bank][:], lhsT=wupr[ri][:, k, wsl], rhs=h2T[:, k, :], start=(k == 0), stop=(k == 7))
                            return r
                        P.op("pe", umm, reads=[bf("wupr%d" % ri)] + [bf("h2T%d" % c) for c in range(4)], writes=[B_pf[bank]])
                        P.op("pool", lambda e, us=us, jj=jj: e.tensor_copy(out=U[us][:, 0:2], in_=carry[:, jj, :]),
                             reads=[bf("carry%d" % jj)], writes=[bf("Uh%d" % us)])
                        P.op("act", lambda e, us=us, bank=bank: e.activation(out=U[us][:, 2:514], in_=pf[bank][:], func=AF.Copy),
                             reads=[B_pf[bank]], writes=[bf("U%d" % us)])
                        P.op("pool", lambda e, us=us, jj=jj: e.tensor_copy(out=carry[:, jj, :], in_=U[us][:, 512:514]),
                             reads=[bf("U%d" % us)], writes=[bf("carry%d" % jj)])
                        P.op("act", lambda e, us=us, jj=jj, bank=bank: e.activation(out=acc[us], in_=pf[bank][:], func=AF.Identity,
                                                                                   scale=cw[:, 2, jj:jj + 1], bias=cbv[:, jj:jj + 1]),
                             reads=[B_pf[bank], bf("cw"), bf("cbv")], writes=[bf("acc%d" % us)])
                        taps.append((us, jj))
                        accs.append(us)
                    for tapk, (c_lo, c_hi) in ((1, (1, 513)), (0, (0, 512))):
                        for (us, jj) in taps:
                            P.op("dve", lambda e, us=us, jj=jj, tapk=tapk, c_lo=c_lo, c_hi=c_hi: e.scalar_tensor_tensor(
                                out=acc[us], in0=U[us][:, c_lo:c_hi], scalar=cw[:, tapk, jj:jj + 1], in1=acc[us], op0=ALU.mult, op1=ALU.add),
                                 reads=[bf("U%d" % us), bf("Uh%d" % us), bf("cw"), bf("acc%d" % us)], writes=[bf("acc%d" % us)])

                    def fin(j=j, accs=tuple(accs)):
                        gi = j % 2
                        P.op("act", lambda e, gi=gi, a0=accs[0]: e.activation(out=gl[gi], in_=acc[a0], func=AF.Gelu_apprx_tanh),
                             reads=[bf("acc%d" % accs[0])], writes=[bf("gl%d" % gi)])
                        P.op("dve", lambda e, gi=gi, a1=accs[1], j=j: e.tensor_tensor(out=yT[:, j, :], in0=gl[gi], in1=acc[a1], op=ALU.mult),
                             reads=[bf("gl%d" % gi), bf("acc%d" % accs[1])], writes=[bf("yT%d" % j)])
                    if deferred:
                        deferred.pop()()
                    deferred.append(fin)
            while deferred:
                deferred.pop()()
            for c in range(4):
                n = 4 * t + c

                d0 = 6 if c % 2 == 0 else 4

                def dmm(e, c=c, d0=d0):
                    r = None
                    for cb in range(2):
                        for j in range(NFB):
                            r = e.matmul(pf[d0 + cb][:], lhsT=yT[:, j, c * 128:(c + 1) * 128], rhs=wdn[:, j, cb * 512:(cb + 1) * 512],
                                         start=(j == 0), stop=(j == NFB - 1))
                    return r
                P.op("pe", dmm, reads=[bf("yT%d" % j) for j in range(NFB)] + [bf("wdn")], writes=[B_pf[d0], B_pf[d0 + 1]])
                c0, c1 = newcol(), newcol()

                def sq3(e, c0=c0, c1=c1, d0=d0):
                    e.activation(out=PT[0][:], in_=pf[d0][:], func=AF.Square, accum_out=ssq[:, c0:c0 + 1])
                    return e.activation(out=PT[1][:], in_=pf[d0 + 1][:], func=AF.Square, accum_out=ssq[:, c1:c1 + 1])
                P.op("act", sq3, reads=[B_pf[d0], B_pf[d0 + 1], bf("ssq")], writes=[bf("PT0"), bf("PT1"), bf("ssqm")])
                P.op("dve", lambda e, c0=c0, c1=c1: e.tensor_tensor(out=stat[:, 58:59], in0=ssq[:, c0:c0 + 1], in1=ssq[:, c1:c1 + 1], op=ALU.add),
                     reads=[bf("ssqm")], writes=[bf("ssqs")])
                P.op("act", lambda e: e.activation(out=stat[:, 59:60], in_=stat[:, 58:59], func=AF.Ln, scale=1.0 / D, bias=stat[:, 63:64]),
                     reads=[bf("ssqs"), bf("epsb")], writes=[bf("lnv2")])
                P.op("act", lambda e: e.activation(out=stat[:, 60:61], in_=stat[:, 59:60], func=AF.Exp, scale=-0.5),
                     reads=[bf("lnv2")], writes=[bf("rstd2")])
                oslot = n % 2

                def post2(e, oslot=oslot, d0=d0):
                    e.scalar_tensor_tensor(out=ob[oslot][:, 0:512], in0=pf[d0][:], scalar=stat[:, 60:61], in1=g_post2[:, 0:512], op0=ALU.mult, op1=ALU.mult)
                    return e.scalar_tensor_tensor(out=ob[oslot][:, 512:1024], in0=pf[d0 + 1][:], scalar=stat[:, 60:61], in1=g_post2[:, 512:1024], op0=ALU.mult, op1=ALU.mult)
                P.op("dve", post2, reads=[B_pf[d0], B_pf[d0 + 1], bf("rstd2"), bf("g_post2")], writes=[bf("ob%d" % oslot)])
                P.op("dve", lambda e, oslot=oslot, c=c: e.tensor_tensor(out=ob[oslot], in0=ob[oslot], in1=xbt[:, c, :], op=ALU.add),
                     reads=[bf("ob%d" % oslot), bf("xbt%d" % c)], writes=[bf("ob%d" % oslot)])
                out_ops.append(P.op("sp", lambda e, n=n, oslot=oslot: [e.dma_start(out=out[n * 128:(n + 1) * 128, :], in_=ob[oslot])],
                                    reads=[bf("ob%d" % oslot)], writes=[bf("out%d" % n)], dsem=osem[oslot], ndma=1))
                if t + 1 < NT:
                    x1_load(t + 1, c)
                    if c >= 1:
                        x1_norm(t + 1, c - 1)
            if t + 1 < NT:
                x1_norm(t + 1, 3)
        if stop < 99:
            P.stopped = False
            s_dbg = dsem("s_dbg")
            out_ops.append(P.op("sp", lambda e: [e.dma_start(out=out[0:128, 0:512], in_=state), e.dma_start(out=out[128:256, :], in_=x1o[0]),
                                                e.dma_start(out=out[256:384, :], in_=x1o[1])],
                                reads=[bf("state"), bf("x1o0"), bf("x1o1")], dsem=s_dbg, ndma=3))
        P.op("sp", None, extra=[o for o in out_ops if getattr(o, 'dsem', None) is not None])

        P.finalize()

        block = es.enter_context(nc.Block())

        @block.sync
        def _(e):
            P.emit_engine("sp", e, esems)

        @block.scalar
        def _(e):
            P.emit_engine("act", e, esems)

        @block.vector
        def _(e):
            P.emit_engine("dve", e, esems)

        @block.gpsimd
        def _(e):
            P.emit_engine("pool", e, esems)

        @block.tensor
        def _(e):
            P.emit_engine("pe", e, esems)
    return nc


def _tables():
    half = np.linspace(0.0, 1.0, 64, dtype=np.float32)
    angle = (np.float32(1.0) / np.power(np.float32(10000.0), half)).astype(np.float32)
    angle = np.repeat(angle, 2)
    pos = np.arange(S, dtype=np.float32)
    arg = (pos[:, None] * angle[None]).astype(np.float32)
    sin = np.sin(arg).astype(np.float64)
    cos = np.cos(arg).astype(np.float64)
    sgn = np.where(np.arange(128) % 2 == 0, -1.0, 1.0)
    sinS = sin * sgn[None]
    i = np.arange(S) % 128
    g = np.array(GAMMA, dtype=np.float64)
    xi = g[None, :] ** (i[:, None] + 1.0)
    zeta = g[None, :] ** (127.0 - i[:, None])
    tq = np.empty((S, 2, 4, 128), np.float32)
    tk = np.empty((S, 2, 4, 128), np.float32)
    ks = 128.0 ** -0.5
    for h in range(4):
        tq[:, 0, h] = cos * xi[:, h:h + 1]
        tq[:, 1, h] = sinS * xi[:, h:h + 1]
        tk[:, 0, h] = cos * zeta[:, h:h + 1] * ks
        tk[:, 1, h] = sinS * zeta[:, h:h + 1] * ks
    return tq.reshape(S, 2, 512), tk.reshape(S, 2, 512)


_CACHE = {}


def kernel(x, mix_pre_norm, w_in, attn_sinks, w_out, mix_post_norm, ffn_pre_norm, w_up, conv_w, conv_b, w_down, ffn_post_norm):
    f32 = np.float32
    x = np.asarray(x, f32)[0]
    if "nc" not in _CACHE:
        import os
        _CACHE["nc"] = build_program(stop=float(os.environ.get("KSTOP", "99")))
        _CACHE["tabs"] = _tables()
    nc = _CACHE["nc"]
    tq, tk = _CACHE["tabs"]
    gains = np.stack([np.asarray(a, f32)[0] for a in (mix_pre_norm, mix_post_norm, ffn_pre_norm, ffn_post_norm)], 0)
    kk = np.arange(128)[:, None]
    qq = np.arange(128)[None, :]
    cur = (kk <= qq).astype(f32)
    prev = (kk > qq).astype(f32)
    rmask = np.concatenate([cur * f32(GAMMA[h] ** -128.0) for h in range(4)], axis=1).astype(f32)
    common = {
        "w_in": np.ascontiguousarray(np.asarray(w_in, f32)[0]),
        "w_out": np.ascontiguousarray(np.asarray(w_out, f32)[0]),
        "w_up": np.ascontiguousarray(np.asarray(w_up, f32)[0]),
        "w_down": np.ascontiguousarray(np.asarray(w_down, f32)[0]),
        "gains": np.ascontiguousarray(gains),
        "sinks": np.ascontiguousarray(np.asarray(attn_sinks, f32)),
        "conv_w": np.ascontiguousarray(np.asarray(conv_w, f32)[0].reshape(3, 44, 128).transpose(2, 0, 1).reshape(128, 132)),
        "conv_b": np.ascontiguousarray(np.asarray(conv_b, f32)[0].reshape(44, 128).T),
        "ident": np.eye(128, dtype=f32),
        "rmask": rmask,
    }
    in_maps = []
    for c in range(NCORES):
        t0 = c * TPC - (NPRE + 1) * 128
        xe = np.zeros((NX * 128, D), f32)
        tke = np.zeros((NX * 128, 2, 512), f32)
        lo = max(t0, 0)
        xe[lo - t0:] = x[lo:(c + 1) * TPC]
        tke[lo - t0:] = tk[lo:(c + 1) * TPC]
        tqe = np.zeros((NM * 128, 2, 512), f32)
        q0 = c * TPC - 128
        ql = max(q0, 0)
        tqe[ql - q0:] = tq[ql:(c + 1) * TPC]
        am = np.stack([np.tile(cur, (1, 4)), np.tile(prev, (1, 4)), np.tile(prev, (1, 4)) * (1.0 if c > 0 else 0.0)], axis=1).astype(f32)
        m = dict(common)
        m["x_ext"] = xe
        m["amask"] = np.ascontiguousarray(am)
        m["tq"] = tqe
        m["tk"] = tke
        m["coef"] = np.zeros((1, 40), f32)
        in_maps.append(m)
    import os
    if os.environ.get("KSAME"):
        in_maps = [in_maps[int(os.environ["KSAME"])]] * NCORES
    res = run_bass_kernel_spmd(nc, in_maps, core_ids=list(range(NCORES)))
    outs = [np.asarray(r["out"], f32) for r in res.results]
    return np.concatenate(outs, axis=0)[None]
```

```python
import math
from contextlib import ExitStack

import numpy as np
import concourse.bass as bass
import concourse.mybir as mybir
from concourse.bass_utils import run_bass_kernel_spmd

F32 = mybir.dt.float32
BF16 = mybir.dt.bfloat16
AF = mybir.ActivationFunctionType
ALU = mybir.AluOpType

NCORES = 8
S = 16384
D = 1024
TPC = S // NCORES
NCH = TPC // 128
IN_W = 2816
DFF = 2816
NFB = DFF // 128
EPS = 1e-6
GAMMA = [1.0 - 2.0 ** (-5.0 - h) for h in range(4)]
CDECAY = [g ** 128 for g in GAMMA]
NPRE = 44
NX = NPRE + NCH + 1
NM = NCH + 1


class Buf:
    __slots__ = ("name", "w", "r")

    def __init__(self, name):
        self.name = name
        self.w = None
        self.r = []


class DSem:
    def __init__(self, h):
        self.h = h
        self.count = 0


class Op:
    __slots__ = ("eng", "emit", "raw", "oth", "sig", "dsem", "dval", "needed", "dinc")

    def __init__(self):
        self.dsem = None


class Prog:
    ENG = ("sp", "act", "dve", "pool", "pe")

    def __init__(self):
        self.ops = {e: [] for e in self.ENG}
        self.dma_since_barrier = []
        self.stopped = False

    def op(self, eng, emit, reads=(), writes=(), dsem=None, ndma=0, extra=(), dinc=16):
        o = Op()
        if self.stopped:
            return o
        o.eng, o.emit, o.sig, o.needed = eng, emit, None, False
        o.dsem = dsem
        o.dval = None
        o.dinc = dinc
        if dsem is not None:
            dsem.count += dinc * ndma
            o.dval = dsem.count
            self.dma_since_barrier.append(o)
        raw, oth = [], []
        for b in reads:
            if b.w is not None:
                raw.append(b.w)
        for b in writes:
            if b.w is not None:
                oth.append(b.w)
            oth.extend(b.r)
        oth.extend(extra)
        o.raw = [d for d in dict.fromkeys(raw) if d is not o]
        o.oth = [d for d in dict.fromkeys(oth) if d is not o and d not in o.raw]
        for b in reads:
            b.r.append(o)
        for b in writes:
            b.w = o
            b.r = []
        self.ops[eng].append(o)
        return o

    def barrier(self):
        if self.stopped:
            return
        lasts = [self.ops[e][-1] for e in self.ENG if self.ops[e]]
        dmas = list(self.dma_since_barrier)
        self.dma_since_barrier = []
        for e in self.ENG:
            self.op(e, None, extra=[x for x in lasts + dmas])

    def finalize(self):
        for e in self.ENG:
            for o in self.ops[e]:
                for d in o.raw:
                    if d.dsem is None and (d.eng != o.eng or o.eng != "pe"):
                        d.needed = True
                for d in o.oth:
                    if d.dsem is None and d.eng != o.eng:
                        d.needed = True
        for e in self.ENG:
            n = 0
            for o in self.ops[e]:
                if o.needed:
                    n += 1
                    o.sig = n

    def emit_engine(self, e, engobj, esems):
        waited = {}

        def wait(sem, val):
            k = id(sem)
            if waited.get(k, 0) >= val:
                return
            waited[k] = val
            engobj.wait_ge(sem, val)

        for o in self.ops[e]:
            for d, israw in [(d, True) for d in o.raw] + [(d, False) for d in o.oth]:
                if d.dsem is not None:
                    wait(d.dsem.h, d.dval)
                elif d.eng == e:
                    if e != "pe" and israw:
                        wait(esems[e], d.sig)
                else:
                    wait(esems[d.eng], d.sig)
            if o.emit is None:
                continue
            r = o.emit(engobj)
            if o.dsem is not None:
                for ins in r:
                    if o.dinc == 1:
                        ins.then_inc(o.dsem.h)
                    else:
                        ins.then_inc(o.dsem.h, o.dinc)
            elif o.needed:
                r.then_inc(esems[e], 1)


def build_program(stop=99):
    nc = bass.Bass("TRN2", target_bir_lowering=False)
    P = Prog()

    def stage(k):
        if stop < k:
            P.stopped = True

    def din(name, shape):
        return nc.dram_tensor(name, shape, F32, kind="ExternalInput").ap()

    x_ext = din("x_ext", [NX * 128, D])
    w_in = din("w_in", [D, IN_W])
    w_out = din("w_out", [D, D])
    w_up = din("w_up", [D, 2 * DFF])
    w_down = din("w_down", [DFF, D])
    gains = din("gains", [4, D])
    sinks = din("sinks", [1, 8])
    conv_w = din("conv_w", [128, 3 * 44])
    conv_b = din("conv_b", [128, 44])
    ident = din("ident", [128, 128])
    amask_d = din("amask", [128, 3, 512])
    rmask_d = din("rmask", [128, 512])
    tq = din("tq", [NM * 128, 2, 512])
    tk = din("tk", [NX * 128, 2, 512])
    coef = din("coef", [1, 40])
    out = nc.dram_tensor("out", [TPC, D], F32, kind="ExternalOutput").ap()
    x1d = nc.dram_tensor("x1d", [TPC, D], F32)

    es = ExitStack()
    with es:
        def sb(name, shape, dt):
            return es.enter_context(nc.sbuf_tensor(name, shape, dt))

        def sem(name):
            return es.enter_context(nc.semaphore(name))

        def dsem(name):
            return DSem(sem(name))

        esems = {e: sem("e_" + e) for e in Prog.ENG}

        WA = 38912
        warena = sb("warena", [128, WA], BF16)
        aarena = sb("aarena", [128, 17408], BF16)
        FA = 14336
        farena = sb("farena", [128, FA], F32)

        def fa(off, n):
            return farena[:, off:off + n]

        identb = sb("identb", [128, 128], BF16)
        wkd = sb("wkd", [128, 8, 2, 128], BF16)
        hT = [sb("hT%d" % i, [128, 8, 128], BF16) for i in range(2)]
        xns = [sb("xn%d" % i, [128, 1024], BF16) for i in range(2)]
        xn = xns[0]
        qaT = sb("qaT", [128, 2, 512], BF16)
        kT = [sb("kT%d" % i, [128, 2, 128], BF16) for i in range(2)]
        vaug = [sb("vaug%d" % i, [128, 2, 128], BF16) for i in range(2)]
        ones64 = sb("ones64", [128, 128], BF16)
        PT = [sb("PT%d" % i, [128, 512], BF16) for i in range(4)]
        PTm = [sb("PTm%d" % i, [128, 512], BF16) for i in range(4)]
        kztmp = [PT[0], PT[1]]
        vrtmp = [PT[2], PT[3]]
        attnT = sb("attnT", [128, 2, 2, 128], BF16)
        qx = sb("qx", [128, 512], BF16)
        qxT = sb("qxT", [128, 4, 128], BF16)
        kzT = sb("kzT", [128, 4, 128], BF16)
        inT = sb("inT", [128, 512], BF16)
        stbf = sb("stbf", [128, 512], BF16)
        gated = sb("gated", [128, 512], BF16)
        retT = sb("retT", [128, 4, 128], BF16)
        amask = sb("amaskb", [128, 3, 512], BF16)
        ssq = sb("ssq", [128, 192], F32)
        xh = sb("xh", [2, 1024], F32)
        stat = sb("stat", [128, 128], F32)
        h2Th = sb("h2Th", [128, 8, 2], BF16)

        pb0 = es.enter_context(nc.psum_tensor("pb0", [128, 1024], BF16))
        pb1 = es.enter_context(nc.psum_tensor("pb1", [128, 1024], BF16))
        pf = [None, None] + [es.enter_context(nc.psum_tensor("pf%d" % i, [128, 512], F32)) for i in range(2, 8)]
        B_pb0, B_pb1 = Buf("pb0"), Buf("pb1")
        B_pf = [None, None] + [Buf("pf%d" % i) for i in range(2, 8)]

        winv = warena[:, 0:22528].rearrange("p (k n) -> p k n", k=8)
        woa = warena[:, 22528:26624].rearrange("p (h n) -> p h n", h=4)
        wor = warena[:, 26624:30720].rearrange("p (h n) -> p h n", h=4)
        wdn = warena[:, 0:22528].rearrange("p (j n) -> p j n", j=NFB)
        wupr = [warena[:, 22528 + i * 4096:22528 + (i + 1) * 4096].rearrange("p (k n) -> p k n", k=8) for i in range(4)]
        kzs = aarena[:, 0:8704].rearrange("p (c n) -> p c n", c=NM)
        vrs = aarena[:, 8704:17408].rearrange("p (c n) -> p c n", c=NM)
        yT = aarena[:, 0:11264].rearrange("p (j n) -> p j n", j=NFB)
        h2T = aarena[:, 11264:15360].rearrange("p (k n) -> p k n", k=8)

        B = {}

        def bf(name):
            if name not in B:
                B[name] = Buf(name)
            return B[name]

        ssq_col = [0]

        def newcol():
            ssq_col[0] += 1
            assert ssq_col[0] < 192
            return ssq_col[0] - 1

        GH = [0, 2, 1, 3]
        w_in_v = w_in.rearrange("(k p) n -> p k n", p=128)

        s_const = dsem("s_const")
        s_w = [dsem("s_w%d" % i) for i in range(6)]

        g_pre1, g_post1 = fa(0, 1024), fa(1024, 1024)
        xc = [fa(2048, 1024), fa(3072, 1024)]
        tab = [fa(4096, 1024).rearrange("p (a n) -> p a n", a=2), fa(5120, 1024).rearrange("p (a n) -> p a n", a=2)]
        Gst = fa(2048, 4096).rearrange("p (r n) -> p r n", r=8)
        t1, t2, sg, eg = fa(6144, 512), fa(6656, 512), fa(7168, 512), fa(7680, 512)
        x1o = [fa(8192, 1024), fa(9216, 1024)]
        esk = fa(10240, 1024).rearrange("p (j n) -> p j n", j=2)
        rmask = fa(11264, 512)
        state = fa(11776, 512)
        rn = fa(12288, 512)
        den = fa(12800, 512)
        tmpm = fa(13312, 1024)
        coefb = stat[:, 64:104]
        es8 = stat[:, 40:48]

        def c_loads(e):
            r = []
            r.append(e.dma_start(out=g_pre1, in_=gains[0:1, :].partition_broadcast(128)[:, 0, :]))
            r.append(e.dma_start(out=g_post1, in_=gains[1:2, :].partition_broadcast(128)[:, 0, :]))
            r.append(e.dma_start(out=rmask, in_=rmask_d[:, :]))
            r.append(e.dma_start(out=coefb, in_=coef.partition_broadcast(128)[:, 0, :]))
            r.append(e.dma_start(out=es8, in_=sinks.partition_broadcast(128)[:, 0, :]))
            return r
        P.op("sp", c_loads, writes=[bf("g_pre1"), bf("g_post1"), bf("rmask"), bf("coefb"), bf("es8")], dsem=s_const, ndma=5)

        def c_loads2(e):
            r = []
            r.append(e.dma_start(out=identb[:], in_=ident[:, :]))
            r.append(e.dma_start(out=amask[:], in_=amask_d[:, :, :]))
            return r
        s_const2 = dsem("s_const2")
        P.op("pool", c_loads2, writes=[bf("identb"), bf("amask")], dsem=s_const2, ndma=2)

        def w_a0(e):
            return [e.dma_start(out=winv[:, :, 1280:2304], in_=w_in_v[:, :, 1280:2304])]
        P.op("pool", w_a0, writes=[bf("win_kv")], dsem=s_w[0], ndma=1)

        def memsets(e):
            e.memset(ssq[:], 0.0)
            e.memset(state, 0.0)
            e.memset(qaT[:], 0.0)
            return e.memset(ones64[:], 1.0)
        P.op("dve", memsets, writes=[bf("ssq"), bf("state"), bf("ones64"), bf("qaT0")])

        def w_a1(e):
            r = [e.dma_start(out=winv[:, :, 0:1280], in_=w_in_v[:, :, 0:1280]),
                 e.dma_start(out=winv[:, :, 2304:2816], in_=w_in_v[:, :, 2304:2816])]
            for j in range(2):
                for dup in range(2):
                    r.append(e.dma_start(out=wkd[:, :, j, dup * 64:(dup + 1) * 64],
                                         in_=w_in_v[:, :, 512 + 64 * j:512 + 64 * (j + 1)]))
            for j in range(2):
                for a in range(2):
                    hl, hu = 4 * j + GH[2 * a], 4 * j + GH[2 * a + 1]
                    r.append(e.dma_start(out=woa[0:64, 2 * j + a, :], in_=w_out[64 * hl:64 * hl + 64, :]))
                    r.append(e.dma_start(out=woa[64:128, 2 * j + a, :], in_=w_out[64 * hu:64 * hu + 64, :]))
            r.append(e.dma_start(out=wor, in_=w_out[512:1024, :].rearrange("(h p) n -> p h n", p=128)))
            return r

        xsem = [dsem("xs0"), dsem("xs1")]
        tsem = [dsem("ts0"), dsem("ts1")]

        def norm_transpose(xrow0, slot, gain_ap, gain_buf, src_dram, load=True, xbuf=None, xap=None, dst=None, dstbuf=None, lsem=None, pbk=0):
            xa = xap if xap is not None else xc[slot]
            xb_ = xbuf if xbuf is not None else bf("xc%d" % slot)
            xs = slot % 2
            xn_ = xns[xs]
            xnb = bf("xn%d" % xs)
            cl, cr = (48, 49) if xs == 0 else (4, 5)
            lnb, rsb = bf("lnv%d" % xs), bf("rstd%d" % xs)
            pbt, pbb = (pb0, B_pb0) if pbk == 0 else (pb1, B_pb1)
            if load:
                P.op("sp", lambda e: [e.dma_start(out=xa, in_=src_dram[xrow0:xrow0 + 128, :])],
                     writes=[xb_], dsem=(lsem if lsem is not None else xsem[slot]), ndma=1)
            col = newcol()
            P.op("act", lambda e: e.activation(out=xn_[:], in_=xa, func=AF.Square, accum_out=ssq[:, col:col + 1]),
                 reads=[xb_, bf("ssq")], writes=[xnb, bf("ssqc%d" % xs)])
            P.op("act", lambda e: e.activation(out=stat[:, cl:cl + 1], in_=ssq[:, col:col + 1], func=AF.Ln, scale=1.0 / D, bias=stat[:, 63:64]),
                 reads=[bf("ssqc%d" % xs), bf("epsb")], writes=[lnb])
            P.op("act", lambda e: e.activation(out=stat[:, cr:cr + 1], in_=stat[:, cl:cl + 1], func=AF.Exp, scale=-0.5),
                 reads=[lnb], writes=[rsb])
            P.op("dve", lambda e: e.scalar_tensor_tensor(out=xn_[:], in0=xa, scalar=stat[:, cr:cr + 1], in1=gain_ap, op0=ALU.mult, op1=ALU.mult),
                 reads=[xb_, rsb, gain_buf], writes=[xnb])

            def tr(e):
                r = None
                for k in range(8):
                    r = e.transpose(out=pbt[:, k * 128:(k + 1) * 128], in_=xn_[:, k * 128:(k + 1) * 128], identity=identb[:])
                return r
            P.op("pe", tr, reads=[xnb, bf("identb")], writes=[pbb])
            d_ap = dst if dst is not None else hT[slot][:]
            d_buf = dstbuf if dstbuf is not None else bf("hT%d" % slot)
            P.op("act", lambda e: e.activation(out=d_ap, in_=pbt[:].rearrange("p (k n) -> p k n", k=8), func=AF.Copy),
                 reads=[pbb], writes=[d_buf])

        def cst(e):
            e.memset(stat[:, 62:63], 1.0)
            return e.memset(stat[:, 63:64], EPS)
        P.op("pool", cst, writes=[bf("epsb"), bf("oneb")])

        t1s, t2s = [t1, rn], [t2, den]
        t1b, t2b = [bf("t1"), bf("rn")], [bf("t2"), bf("den")]

        def rotate_ops(psbuf, psap, tslot, out_ap, out_buf, ts=0):
            tb = bf("tab%d" % tslot)
            t1_, t2_ = t1s[ts], t2s[ts]
            P.op("dve", lambda e: e.tensor_tensor(out=t1_, in0=psap, in1=tab[tslot][:, 0, :], op=ALU.mult),
                 reads=[psbuf, tb], writes=[t1b[ts]])

            def sw(e):
                pv = psap.rearrange("p (i two) -> p i two", two=2)
                sv = tab[tslot][:, 1, :].rearrange("p (i two) -> p i two", two=2)
                ov = t2_.rearrange("p (i two) -> p i two", two=2)
                e.tensor_tensor(out=ov[:, :, 0], in0=pv[:, :, 1], in1=sv[:, :, 0], op=ALU.mult)
                return e.tensor_tensor(out=ov[:, :, 1], in0=pv[:, :, 0], in1=sv[:, :, 1], op=ALU.mult)
            P.op("dve", sw, reads=[psbuf, tb], writes=[t2b[ts]])
            P.op("dve", lambda e: e.tensor_tensor(out=out_ap, in0=t1_, in1=t2_, op=ALU.add),
                 reads=[t1b[ts], t2b[ts]], writes=[out_buf])

        def kv_state_update(kap, vap, kbuf, vbuf, kb_=4):
            def kvmm(e):
                r = None
                for h in range(4):
                    r = e.matmul(pf[kb_][:, h * 128:(h + 1) * 128], lhsT=kap[:, h * 128:(h + 1) * 128],
                                 rhs=vap[:, h * 128:(h + 1) * 128], start=True, stop=True)
                return r
            P.op("pe", kvmm, reads=[kbuf, vbuf], writes=[B_pf[kb_]])

            def upd(e):
                r = None
                for h in range(4):
                    sl = slice(h * 128, (h + 1) * 128)
                    r = e.scalar_tensor_tensor(out=state[:, sl], in0=state[:, sl], scalar=float(CDECAY[h]), in1=pf[kb_][:, sl],
                                               op0=ALU.mult, op1=ALU.add)
                return r
            P.op("dve", upd, reads=[B_pf[kb_], bf("state")], writes=[bf("state")])

        stage(1)
        def lnrs(xs):
            return ((48, 49) if xs == 0 else (4, 5)), bf("lnv%d" % xs), bf("rstd%d" % xs)

        def pre_s1(i):
            slot = i % 2
            xa, xb_, xn_, xnb = xc[slot], bf("xc%d" % slot), xns[slot], bf("xn%d" % slot)
            (cl, cr), lnb, rsb = lnrs(slot)
            P.op("sp", lambda e: [e.dma_start(out=xa, in_=x_ext[128 * i:128 * i + 128, :])], writes=[xb_], dsem=xsem[slot], ndma=1)
            if i == 0:
                P.op("pool", w_a1, writes=[bf("win_rest"), bf("wkd"), bf("wout")], dsem=s_w[1], ndma=15)
            col = newcol()
            P.op("act", lambda e: e.activation(out=xn_[:], in_=xa, func=AF.Square, accum_out=ssq[:, col:col + 1]),
                 reads=[xb_, bf("ssq")], writes=[xnb, bf("ssqc%d" % slot)])
            P.op("act", lambda e: e.activation(out=stat[:, cl:cl + 1], in_=ssq[:, col:col + 1], func=AF.Ln, scale=1.0 / D, bias=stat[:, 63:64]),
                 reads=[bf("ssqc%d" % slot), bf("epsb")], writes=[lnb])
            P.op("act", lambda e: e.activation(out=stat[:, cr:cr + 1], in_=stat[:, cl:cl + 1], func=AF.Exp, scale=-0.5),
                 reads=[lnb], writes=[rsb])
            P.op("dve", lambda e: e.scalar_tensor_tensor(out=xn_[:], in0=xa, scalar=stat[:, cr:cr + 1], in1=g_pre1, op0=ALU.mult, op1=ALU.mult),
                 reads=[xb_, rsb, bf("g_pre1")], writes=[xnb])

        def pre_s2(i):
            slot = i % 2
            xn_, xnb = xns[slot], bf("xn%d" % slot)
            pbt, pbb = (pb0, B_pb0) if slot == 0 else (pb1, B_pb1)

            def tr(e):
                r = None
                for k in range(8):
                    r = e.transpose(out=pbt[:, k * 128:(k + 1) * 128], in_=xn_[:, k * 128:(k + 1) * 128], identity=identb[:])
                return r
            P.op("pe", tr, reads=[xnb, bf("identb")], writes=[pbb])
            P.op("act", lambda e: e.activation(out=hT[slot][:], in_=pbt[:].rearrange("p (k n) -> p k n", k=8), func=AF.Copy),
                 reads=[pbb], writes=[bf("hT%d" % slot)])

        def pre_kv(i):
            slot = i % 2
            if i < NPRE:
                return kztmp[slot][:], vrtmp[slot][:], bf("kztmp%d" % slot), bf("vrtmp%d" % slot)
            m = i - NPRE
            return kzs[:, m, :], vrs[:, m, :], bf("kz%d" % m), bf("vr%d" % m)

        def pre_s3(i):
            slot = i % 2
            bk, bv = (3, 5) if slot == 0 else (6, 7)

            def proj(e):
                r = None
                for (bank, c0) in ((bk, 1280), (bv, 1792)):
                    for k in range(8):
                        r = e.matmul(pf[bank][:], lhsT=hT[slot][:, k, :], rhs=winv[:, k, c0:c0 + 512], start=(k == 0), stop=(k == 7))
                return r
            P.op("pe", proj, reads=[bf("hT%d" % slot), bf("win_kv")], writes=[B_pf[bk], B_pf[bv]])
            kap, vap, kb, vb = pre_kv(i)
            P.op("act", lambda e: e.activation(out=vap, in_=pf[bv][:], func=AF.Copy), reads=[B_pf[bv]], writes=[vb])
            rotate_ops(B_pf[bk], pf[bk][:], slot, kap, kb, ts=slot)

        def pre_s4(i):
            if i >= NPRE:
                return
            kap, vap, kb, vb = pre_kv(i)
            kv_state_update(kap, vap, kb, vb, kb_=(4 if i % 2 == 0 else 2))

        for j in range(-2, NX + 1):
            if 0 <= j + 2 < NX:
                pre_s1(j + 2)
            if 0 <= j + 1 < NX:
                pre_s2(j + 1)
            if 0 <= j < NX:
                pre_s3(j)
            if 0 <= j - 1 < NX:
                pre_s4(j - 1)
            if 0 <= j + 2 < NX:
                P.op("pool", lambda e, i=j + 2: [e.dma_start(out=tab[i % 2], in_=tk[i * 128:(i + 1) * 128, :, :])],
                     writes=[bf("tab%d" % ((j + 2) % 2))], dsem=tsem[(j + 2) % 2], ndma=1)

        stage(2)
        stage(3)
        P.op("act", lambda e: e.activation(out=es8, in_=es8, func=AF.Exp), reads=[bf("es8")], writes=[bf("es8")])

        def esk_fill(e):
            r = None
            for j in range(2):
                for gs in range(4):
                    hq = 4 * j + GH[gs]
                    r = e.activation(out=esk[:, j, gs * 128:(gs + 1) * 128], in_=rmask[:, 0:128], func=AF.Identity, scale=0.0, bias=es8[:, hq:hq + 1])
            return r
        P.op("act", esk_fill, reads=[bf("es8"), bf("rmask")], writes=[bf("esk")])

        def kv_from_hT(slot_h, slot_kv, load_v=True):
            def mm(e):
                r = None
                for j in range(2):
                    for k in range(8):
                        r = e.matmul(pf[3][:, j * 128:(j + 1) * 128], lhsT=wkd[:, k, j, :], rhs=hT[slot_h][:, k, :], start=(k == 0), stop=(k == 7))
                for k in range(8):
                    r = e.matmul(pf[3][:, 256:384], lhsT=hT[slot_h][:, k, :], rhs=winv[:, k, 640:768], start=(k == 0), stop=(k == 7))
                return r
            P.op("pe", mm, reads=[bf("hT%d" % slot_h), bf("wkd"), bf("win_rest")], writes=[B_pf[3]])
            P.op("act", lambda e: e.activation(out=kT[slot_kv][:], in_=pf[3][:, 0:256].rearrange("p (j n) -> p j n", j=2), func=AF.Copy),
                 reads=[B_pf[3]], writes=[bf("kT%d" % slot_kv)])
            def vevac(e):
                e.activation(out=vaug[slot_kv][:, :, 0:64], in_=pf[3][:, 256:384].rearrange("p (j n) -> p j n", j=2), func=AF.Copy)
                return e.activation(out=vaug[slot_kv][:, :, 64:128], in_=pf[3][:, 256:384].rearrange("p (j n) -> p j n", j=2), func=AF.Copy)
            P.op("act", vevac, reads=[B_pf[3]], writes=[bf("va%d" % slot_kv)])

        norm_transpose(128 * (NPRE - 1), 0, g_pre1, bf("g_pre1"), x_ext)
        kv_from_hT(0, 1)

        def main_head(m):
            slot = m % 2
            P.op("pool", lambda e, m=m, slot=slot: [e.dma_start(out=tab[slot], in_=tq[m * 128:(m + 1) * 128, :, :])],
                 writes=[bf("tab%d" % slot)], dsem=tsem[slot], ndma=1)
            norm_transpose(128 * (NPRE + m), slot, g_pre1, bf("g_pre1"), x_ext)

        x1sem = [dsem("x1s0"), dsem("x1s1")]
        s_hb = dsem("s_hb")
        for m in range(NM):
            n = m - 1
            if m >= 1:
                stage(4 + m * 0.01)
            slot = m % 2
            cur, prv = m % 2, (m + 1) % 2
            if m == 0:
                main_head(0)

            def proj_q(e, slot=slot):
                r = None
                for pr in range(4):
                    for k in range(8):
                        r = e.matmul(pf[2][:, pr * 128:(pr + 1) * 128], lhsT=winv[:, k, pr * 128:(pr + 1) * 128], rhs=hT[slot][:, k, :],
                                     start=(k == 0), stop=(k == 7))
                return r
            P.op("pe", proj_q, reads=[bf("hT%d" % slot), bf("win_rest")], writes=[B_pf[2]])
            def qevac(e):
                e.activation(out=qaT[0:64, :, 0:256], in_=pf[2][0:64, :].rearrange("p (j n) -> p j n", j=2), func=AF.Copy)
                return e.activation(out=qaT[64:128, :, 256:512], in_=pf[2][64:128, :].rearrange("p (j n) -> p j n", j=2), func=AF.Copy)
            P.op("act", qevac, reads=[B_pf[2], bf("qaT0")], writes=[bf("qaT")])
            kv_from_hT(slot, cur)

            def proj_r(e, slot=slot):
                r = None
                for (bank, c0) in ((4, 768), (5, 2304)):
                    for k in range(8):
                        r = e.matmul(pf[bank][:], lhsT=hT[slot][:, k, :], rhs=winv[:, k, c0:c0 + 512], start=(k == 0), stop=(k == 7))
                return r
            P.op("pe", proj_r, reads=[bf("hT%d" % slot), bf("win_rest")], writes=[B_pf[4], B_pf[5]])
            rotate_ops(B_pf[4], pf[4][:], slot, qx[:], bf("qx"))
            P.op("act", lambda e: e.activation(out=eg, in_=pf[5][:], func=AF.Exp, scale=-1.0), reads=[B_pf[5]], writes=[bf("eg")])
            P.op("act", lambda e: e.activation(out=eg, in_=eg, func=AF.Ln, scale=1.0, bias=stat[:, 62:63]), reads=[bf("eg"), bf("oneb")], writes=[bf("eg")])
            P.op("act", lambda e: e.activation(out=eg, in_=eg, func=AF.Exp, scale=-1.0), reads=[bf("eg")], writes=[bf("eg")])
            P.op("dve", lambda e: e.tensor_tensor(out=sg, in0=eg, in1=pf[5][:], op=ALU.mult), reads=[bf("eg"), B_pf[5]], writes=[bf("sg")])

            stage(3.2 if m == 0 else 4 + m * 0.01 + 0.002)
            if m + 1 < NM:
                main_head(m + 1)
            first = (m == 1)
            for j in range(2):
                for bi, (kslot, mi) in enumerate(((prv, 2 if first else 1), (cur, 0))):
                    ti = j * 2 + bi
                    bank = 6 + (ti % 2)

                    def sc(e, j=j, kslot=kslot, bank=bank):
                        return e.matmul(pf[bank][:], lhsT=kT[kslot][:, j, :], rhs=qaT[:, j, :], start=True, stop=True)
                    P.op("pe", sc, reads=[bf("kT%d" % kslot), bf("qaT")], writes=[B_pf[bank]])
                    P.op("act", lambda e, ti=ti, bank=bank: e.activation(out=PT[ti][:], in_=pf[bank][:], func=AF.Exp, scale=0.125),
                         reads=[B_pf[bank]], writes=[bf("PT%d" % ti)])
                    P.op("dve", lambda e, ti=ti, mi=mi: e.tensor_tensor(out=PTm[ti][:], in0=PT[ti][:], in1=amask[:, mi, :], op=ALU.mult),
                         reads=[bf("PT%d" % ti), bf("amask")], writes=[bf("PTm%d" % ti)])
            stage(3.4 if m == 0 else 4 + m * 0.01 + 0.004)
            def trqk(e, n=m):
                r = None
                for h in range(4):
                    r = e.transpose(out=pb1[:, h * 128:(h + 1) * 128], in_=qx[:, h * 128:(h + 1) * 128], identity=identb[:])
                for h in range(4):
                    r = e.transpose(out=pb1[:, 512 + h * 128:512 + (h + 1) * 128], in_=kzs[:, n, h * 128:(h + 1) * 128], identity=identb[:])
                return r
            P.op("pe", trqk, reads=[bf("qx"), bf("kz%d" % m), bf("identb")], writes=[B_pb1])
            P.op("act", lambda e: e.activation(out=qxT[:], in_=pb1[:, 0:512].rearrange("p (h n) -> p h n", h=4), func=AF.Copy),
                 reads=[B_pb1], writes=[bf("qxT")])
            P.op("act", lambda e: e.activation(out=kzT[:], in_=pb1[:, 512:1024].rearrange("p (h n) -> p h n", h=4), func=AF.Copy),
                 reads=[B_pb1], writes=[bf("kzT")])

            def inner(e):
                r = None
                for h in range(4):
                    r = e.matmul(pf[4][:, h * 128:(h + 1) * 128], lhsT=kzT[:, h, :], rhs=qxT[:, h, :], start=True, stop=True)
                return r
            P.op("pe", inner, reads=[bf("kzT"), bf("qxT")], writes=[B_pf[4]])
            P.op("dve", lambda e: e.tensor_tensor(out=inT[:], in0=pf[4][:], in1=rmask, op=ALU.mult),
                 reads=[B_pf[4], bf("rmask")], writes=[bf("inT")])
            P.op("pool", lambda e: e.tensor_copy(out=stbf[:], in_=state), reads=[bf("state")], writes=[bf("stbf")])

            if m < NM - 1:
                kv_state_update(kzs[:, m, :], vrs[:, m, :], bf("kz%d" % m), bf("vr%d" % m))
            for j in range(2):
                def pv(e, j=j, prv=prv, cur=cur):
                    e.matmul(pf[2][:], lhsT=vaug[prv][:, j, :], rhs=PTm[2 * j][:], start=True, stop=False)
                    e.matmul(pf[2][:], lhsT=vaug[cur][:, j, :], rhs=PTm[2 * j + 1][:], start=False, stop=True)
                    e.matmul(pf[3][:], lhsT=ones64[:], rhs=PTm[2 * j][:], start=True, stop=False)
                    return e.matmul(pf[3][:], lhsT=ones64[:], rhs=PTm[2 * j + 1][:], start=False, stop=True)
                P.op("pe", pv, reads=[bf("va%d" % prv), bf("va%d" % cur), bf("PTm%d" % (2 * j)), bf("PTm%d" % (2 * j + 1)), bf("ones64")],
                     writes=[B_pf[2], B_pf[3]])
                def lnden(e, j=j):
                    r = None
                    for gs in range(4):
                        hq = 4 * j + GH[gs]
                        r = e.activation(out=den[:, gs * 128:(gs + 1) * 128], in_=pf[3][:, gs * 128:(gs + 1) * 128], func=AF.Ln,
                                         scale=1.0, bias=es8[:, hq:hq + 1])
                    return r
                P.op("act", lnden, reads=[B_pf[3], bf("es8")], writes=[bf("den")])
                P.op("act", lambda e: e.activation(out=den, in_=den, func=AF.Exp, scale=-1.0), reads=[bf("den")], writes=[bf("den")])

                def anorm(e, j=j):
                    ov = pf[2][:].rearrange("p (a b q) -> p a b q", a=2, b=2)
                    dv = den.rearrange("p (a b q) -> p a b q", a=2, b=2)
                    e.tensor_tensor(out=attnT[0:64, j, :, :], in0=ov[0:64, :, 0, :], in1=dv[0:64, :, 0, :], op=ALU.mult)
                    return e.tensor_tensor(out=attnT[64:128, j, :, :], in0=ov[64:128, :, 1, :], in1=dv[64:128, :, 1, :], op=ALU.mult)
                P.op("dve", anorm, reads=[B_pf[2], bf("den")], writes=[bf("attnT%d" % j)])

            def omm(e, n=m):
                r = None
                for h in range(4):
                    sl = slice(h * 128, (h + 1) * 128)
                    e.matmul(pf[6][:, sl], lhsT=inT[:, sl], rhs=vrs[:, n, sl], start=True, stop=False)
                    r = e.matmul(pf[6][:, sl], lhsT=qxT[:, h, :], rhs=stbf[:, sl], start=False, stop=True)
                return r
            P.op("pe", omm, reads=[bf("inT"), bf("vr%d" % m), bf("qxT"), bf("stbf")], writes=[B_pf[6]])

            def gn_stats(e):
                r = None
                for h in range(4):
                    r = e.bn_stats(out=stat[:, 8 + 6 * h:14 + 6 * h], in_=pf[6][:, h * 128:(h + 1) * 128])
                return r
            P.op("dve", gn_stats, reads=[B_pf[6]], writes=[bf("bnst")])

            def gn_aggr(e):
                r = None
                for h in range(4):
                    r = e.bn_aggr(out=stat[:, 32 + 2 * h:34 + 2 * h], in_=stat[:, 8 + 6 * h:14 + 6 * h])
                return r
            P.op("dve", gn_aggr, reads=[bf("bnst")], writes=[bf("mv")])
            mvv = stat[:, 32:40].rearrange("p (h two) -> p h two", two=2)
            P.op("act", lambda e: e.activation(out=stat[:, 0:4], in_=mvv[:, :, 1], func=AF.Ln, scale=1.0, bias=stat[:, 63:64]),
                 reads=[bf("mv"), bf("epsb")], writes=[bf("glnv")])
            P.op("act", lambda e: e.activation(out=stat[:, 50:54], in_=stat[:, 0:4], func=AF.Exp, scale=-0.5),
                 reads=[bf("glnv")], writes=[bf("grstd")])
            P.op("dve", lambda e: e.scalar_tensor_tensor(out=stat[:, 54:58], in0=mvv[:, :, 0], scalar=-1.0, in1=stat[:, 50:54], op0=ALU.mult, op1=ALU.mult),
                 reads=[bf("mv"), bf("grstd")], writes=[bf("gnmr")])

            def gn_apply(e):
                r = None
                for h in range(4):
                    sl = slice(h * 128, (h + 1) * 128)
                    r = e.activation(out=rn[:, sl], in_=pf[6][:, sl], func=AF.Identity, scale=stat[:, 50 + h:51 + h], bias=stat[:, 54 + h:55 + h])
                return r
            P.op("act", gn_apply, reads=[B_pf[6], bf("grstd"), bf("gnmr")], writes=[bf("rn")])
            P.op("dve", lambda e: e.tensor_tensor(out=gated[:], in0=rn, in1=sg, op=ALU.mult), reads=[bf("rn"), bf("sg")], writes=[bf("gated")])

            def trr(e):
                r = None
                for h in range(4):
                    r = e.transpose(out=pb1[:, h * 128:(h + 1) * 128], in_=gated[:, h * 128:(h + 1) * 128], identity=identb[:])
                return r
            P.op("pe", trr, reads=[bf("gated"), bf("identb")], writes=[B_pb1])
            P.op("act", lambda e: e.activation(out=retT[:], in_=pb1[:, 0:512].rearrange("p (h n) -> p h n", h=4), func=AF.Copy),
                 reads=[B_pb1], writes=[bf("retT")])

            stage(3.6 if m == 0 else 4 + m * 0.01 + 0.006)
            def wout(e):
                r = None
                for cb in range(2):
                    bank = 6 + cb
                    cs = slice(cb * 512, (cb + 1) * 512)
                    first_mm = True
                    for j in range(2):
                        for a in range(2):
                            e.matmul(pf[bank][:], lhsT=attnT[:, j, a, :], rhs=woa[:, 2 * j + a, cs], start=first_mm, stop=False)
                            first_mm = False
                    for h in range(4):
                        r = e.matmul(pf[bank][:], lhsT=retT[:, h, :], rhs=wor[:, h, cs], start=False, stop=(h == 3))
                return r
            P.op("pe", wout, reads=[bf("attnT0"), bf("attnT1"), bf("retT"), bf("wout")], writes=[B_pf[6], B_pf[7]])

            stage(3.8 if m == 0 else 4 + m * 0.01 + 0.008)
            c0, c1 = newcol(), newcol()

            def sq2(e, c0=c0, c1=c1):
                e.activation(out=PT[0][:], in_=pf[6][:], func=AF.Square, accum_out=ssq[:, c0:c0 + 1])
                return e.activation(out=PT[1][:], in_=pf[7][:], func=AF.Square, accum_out=ssq[:, c1:c1 + 1])
            P.op("act", sq2, reads=[B_pf[6], B_pf[7], bf("ssq")], writes=[bf("PT0"), bf("PT1"), bf("ssqm")])
            P.op("dve", lambda e, c0=c0, c1=c1: e.tensor_tensor(out=stat[:, 58:59], in0=ssq[:, c0:c0 + 1], in1=ssq[:, c1:c1 + 1], op=ALU.add),
                 reads=[bf("ssqm")], writes=[bf("ssqs")])
            P.op("act", lambda e: e.activation(out=stat[:, 59:60], in_=stat[:, 58:59], func=AF.Ln, scale=1.0 / D, bias=stat[:, 63:64]),
                 reads=[bf("ssqs"), bf("epsb")], writes=[bf("lnv2")])
            P.op("act", lambda e: e.activation(out=stat[:, 60:61], in_=stat[:, 59:60], func=AF.Exp, scale=-0.5),
                 reads=[bf("lnv2")], writes=[bf("rstd2")])

            def post(e):
                e.scalar_tensor_tensor(out=tmpm[:, 0:512], in0=pf[6][:], scalar=stat[:, 60:61], in1=g_post1[:, 0:512], op0=ALU.mult, op1=ALU.mult)
                return e.scalar_tensor_tensor(out=tmpm[:, 512:1024], in0=pf[7][:], scalar=stat[:, 60:61], in1=g_post1[:, 512:1024], op0=ALU.mult, op1=ALU.mult)
            P.op("dve", post, reads=[B_pf[6], B_pf[7], bf("rstd2"), bf("g_post1")], writes=[bf("tmpm")])
            oslot = m % 2
            P.op("dve", lambda e, slot=slot, oslot=oslot: e.tensor_tensor(out=x1o[oslot], in0=tmpm, in1=xc[slot], op=ALU.add),
                 reads=[bf("tmpm"), bf("xc%d" % slot)], writes=[bf("x1o%d" % oslot)])
            if m == 0:
                P.op("sp", lambda e, oslot=oslot: [e.dma_start(out=xh[:], in_=x1o[oslot][126:128, :])],
                     reads=[bf("x1o%d" % oslot)], writes=[bf("xh")], dsem=s_hb, ndma=1)
            else:
                P.op("sp", lambda e, n=n, oslot=oslot: [e.dma_start(out=x1d[n * 128:(n + 1) * 128, :], in_=x1o[oslot])],
                     reads=[bf("x1o%d" % oslot)], writes=[bf("x1d%d" % n)], dsem=x1sem[oslot], ndma=1)

        stage(5)
        P.barrier()
        stage(6)

        g_pre2, g_post2 = fa(0, 1024), fa(1024, 1024)
        xbt = fa(2048, 4096).rearrange("p (c n) -> p c n", c=4)
        U = [fa(6144 + i * 516, 516) for i in range(4)]
        acc = [fa(8208 + i * 512, 512) for i in range(4)]
        gl = [fa(10256 + i * 512, 512) for i in range(2)]
        ob = [fa(11280 + i * 1024, 1024) for i in range(2)]
        carry = fa(13328, 88).rearrange("p (j t) -> p j t", t=2)
        cw = fa(13416, 132).rearrange("p (k j) -> p k j", k=3)
        cbv = fa(13548, 44)

        s_b = dsem("s_b")

        def b_loads(e):
            r = []
            r.append(e.dma_start(out=g_pre2, in_=gains[2:3, :].partition_broadcast(128)[:, 0, :]))
            r.append(e.dma_start(out=g_post2, in_=gains[3:4, :].partition_broadcast(128)[:, 0, :]))
            r.append(e.dma_start(out=fa(13416, 132), in_=conv_w[:, :]))
            r.append(e.dma_start(out=cbv, in_=conv_b[:, :]))
            return r
        P.op("sp", b_loads, writes=[bf("g_pre2"), bf("g_post2"), bf("cw"), bf("cbv")], dsem=s_b, ndma=4)
        s_wd = dsem("s_wd")
        w_down_v = w_down.rearrange("(j p) n -> p j n", p=128)
        w_up_v = w_up.rearrange("(k p) n -> p k n", p=128)

        NPC = 6
        pw = [512] * 5 + [256]
        ring_sem = [dsem("s_up%d" % i) for i in range(4)]
        ring_state = {"i": 0}

        def load_piece(part, q):
            i = ring_state["i"] % 4
            ring_state["i"] += 1
            c0 = part * DFF + q * 512
            w = pw[q]
            P.op("pool", lambda e, i=i, c0=c0, w=w: [e.dma_start(out=wupr[i][:, :, 0:w], in_=w_up_v[:, :, c0:c0 + w])],
                 writes=[bf("wupr%d" % i)], dsem=ring_sem[i], ndma=1)
            return i

        seq = [(q, part) for q in range(NPC) for part in (0, 1)]

        colh = newcol()
        P.op("act", lambda e: e.activation(out=xns[1][0:2, :], in_=xh[:], func=AF.Square, accum_out=ssq[0:2, colh:colh + 1]),
             reads=[bf("xh"), bf("ssq")], writes=[bf("xn1"), bf("ssqc")])
        P.op("act", lambda e: e.activation(out=stat[0:2, 48:49], in_=ssq[0:2, colh:colh + 1], func=AF.Ln, scale=1.0 / D, bias=stat[0:2, 63:64]),
             reads=[bf("ssqc"), bf("epsb")], writes=[bf("lnv")])
        P.op("act", lambda e: e.activation(out=stat[0:2, 49:50], in_=stat[0:2, 48:49], func=AF.Exp, scale=-0.5),
             reads=[bf("lnv")], writes=[bf("rstd")])
        P.op("dve", lambda e: e.memset(xn[:], 0.0), writes=[bf("xn0")])
        P.op("dve", lambda e: e.scalar_tensor_tensor(out=xn[0:2, :], in0=xh[:], scalar=stat[0:2, 49:50], in1=g_pre2[0:2, :], op0=ALU.mult, op1=ALU.mult),
             reads=[bf("xh"), bf("rstd"), bf("g_pre2"), bf("xn0")], writes=[bf("xn0")])

        def trh(e):
            r = None
            for k in range(8):
                r = e.transpose(out=pb0[:, k * 128:(k + 1) * 128], in_=xn[:, k * 128:(k + 1) * 128], identity=identb[:])
            return r
        P.op("pe", trh, reads=[bf("xn0"), bf("identb")], writes=[B_pb0])
        P.op("act", lambda e: e.activation(out=h2Th[:], in_=pb0[:].rearrange("p (k n) -> p k n", k=8)[:, :, 0:2], func=AF.Copy),
             reads=[B_pb0], writes=[bf("h2Th")])

        xbsem = [dsem("xb%d" % i) for i in range(4)]
        osem = [dsem("os0"), dsem("os1")]
        out_ops = []
        NT = NCH // 4
        for t in range(NT):
            if t == 1:
                stage(7)
            for c in range(4):
                n = 4 * t + c
                norm_transpose(128 * n, c % 2, g_pre2, bf("g_pre2"), x1d.ap(), load=True, xbuf=bf("xbt%d" % c), xap=xbt[:, c, :],
                               dst=h2T[:, :, c * 128:(c + 1) * 128], dstbuf=bf("h2T%d" % c), lsem=xbsem[c])
            if t == 0:
                P.op("pool", lambda e: [e.dma_start(out=wdn[:, a:b, :], in_=w_down_v[:, a:b, :]) for (a, b) in ((0, 6), (6, 12), (12, 18), (18, 22))],
                     writes=[bf("wdn")], dsem=s_wd, ndma=4)
            pieces = {}
            pending = list(seq)
            for _ in range(3):
                q, part = pending.pop(0)
                pieces[(q, part)] = load_piece(part, q)
            ui = 0
            deferred = []
            for q in range(NPC):
                nb = pw[q] // 128
                for _ in range(1 if q == 0 else 2):
                    if pending:
                        q2, p2 = pending.pop(0)
                        pieces[(q2, p2)] = load_piece(p2, q2)
                for jb in range(nb):
                    j = q * 4 + jb
                    accs = []
                    taps = []
                    for part in (0, 1):
                        ri = pieces[(q, part)]
                        jj = part * NFB + j
                        bank = 2 + (ui % 4)
                        us = ui % 4
                        ui += 1
                        wsl = slice(jb * 128, (jb + 1) * 128)
                        if t == 0:
                            def cmm(e, ri=ri, wsl=wsl):
                                r = None
                                for k in range(8):
                                    r = e.matmul(pf[6][:, 0:2], lhsT=wupr[ri][:, k, wsl], rhs=h2Th[:, k, :], start=(k == 0), stop=(k == 7))
                                return r
                            P.op("pe", cmm, reads=[bf("wupr%d" % ri), bf("h2Th")], writes=[B_pf[6]])
                            P.op("act", lambda e, jj=jj: e.activation(out=carry[:, jj, :], in_=pf[6][:, 0:2], func=AF.Copy),
                                 reads=[B_pf[6]], writes=[bf("carry%d" % jj)])

                        def umm(e, ri=ri, wsl=wsl, bank=bank):
                            r = None
                            for k in range(8):
                                r = e.matmul(pf[bank][:], lhsT=wupr[ri][:, k, wsl], rhs=h2T[:, k, :], start=(k == 0), stop=(k == 7))
                            return r
                        P.op("pe", umm, reads=[bf("wupr%d" % ri)] + [bf("h2T%d" % c) for c in range(4)], writes=[B_pf[bank]])
                        P.op("pool", lambda e, us=us, jj=jj: e.tensor_copy(out=U[us][:, 0:2], in_=carry[:, jj, :]),
                             reads=[bf("carry%d" % jj)], writes=[bf("Uh%d" % us)])
                        P.op("act", lambda e, us=us, bank=bank: e.activation(out=U[us][:, 2:514], in_=pf[bank][:], func=AF.Copy),
                             reads=[B_pf[bank]], writes=[bf("U%d" % us)])
                        P.op("pool", lambda e, us=us, jj=jj: e.tensor_copy(out=carry[:, jj, :], in_=U[us][:, 512:514]),
                             reads=[bf("U%d" % us)], writes=[bf("carry%d" % jj)])
                        P.op("act", lambda e, us=us, jj=jj, bank=bank: e.activation(out=acc[us], in_=pf[bank][:], func=AF.Identity,
                                                                                   scale=cw[:, 2, jj:jj + 1], bias=cbv[:, jj:jj + 1]),
                             reads=[B_pf[bank], bf("cw"), bf("cbv")], writes=[bf("acc%d" % us)])
                        taps.append((us, jj))
                        accs.append(us)
                    for tapk, (c_lo, c_hi) in ((1, (1, 513)), (0, (0, 512))):
                        for (us, jj) in taps:
                            P.op("dve", lambda e, us=us, jj=jj, tapk=tapk, c_lo=c_lo, c_hi=c_hi: e.scalar_tensor_tensor(
                                out=acc[us], in0=U[us][:, c_lo:c_hi], scalar=cw[:, tapk, jj:jj + 1], in1=acc[us], op0=ALU.mult, op1=ALU.add),
                                 reads=[bf("U%d" % us), bf("Uh%d" % us), bf("cw"), bf("acc%d" % us)], writes=[bf("acc%d" % us)])

                    def fin(j=j, accs=tuple(accs)):
                        gi = j % 2
                        P.op("act", lambda e, gi=gi, a0=accs[0]: e.activation(out=gl[gi], in_=acc[a0], func=AF.Gelu_apprx_tanh),
                             reads=[bf("acc%d" % accs[0])], writes=[bf("gl%d" % gi)])
                        P.op("dve", lambda e, gi=gi, a1=accs[1], j=j: e.tensor_tensor(out=yT[:, j, :], in0=gl[gi], in1=acc[a1], op=ALU.mult),
                             reads=[bf("gl%d" % gi), bf("acc%d" % accs[1])], writes=[bf("yT%d" % j)])
                    if deferred:
                        deferred.pop()()
                    deferred.append(fin)
            while deferred:
                deferred.pop()()
            for c in range(4):
                n = 4 * t + c

                d0 = 6 if c % 2 == 0 else 4

                def dmm(e, c=c, d0=d0):
                    r = None
                    for cb in range(2):
                        for j in range(NFB):
                            r = e.matmul(pf[d0 + cb][:], lhsT=yT[:, j, c * 128:(c + 1) * 128], rhs=wdn[:, j, cb * 512:(cb + 1) * 512],
                                         start=(j == 0), stop=(j == NFB - 1))
                    return r
                P.op("pe", dmm, reads=[bf("yT%d" % j) for j in range(NFB)] + [bf("wdn")], writes=[B_pf[d0], B_pf[d0 + 1]])
                c0, c1 = newcol(), newcol()

                def sq3(e, c0=c0, c1=c1, d0=d0):
                    e.activation(out=PT[0][:], in_=pf[d0][:], func=AF.Square, accum_out=ssq[:, c0:c0 + 1])
                    return e.activation(out=PT[1][:], in_=pf[d0 + 1][:], func=AF.Square, accum_out=ssq[:, c1:c1 + 1])
                P.op("act", sq3, reads=[B_pf[d0], B_pf[d0 + 1], bf("ssq")], writes=[bf("PT0"), bf("PT1"), bf("ssqm")])
                P.op("dve", lambda e, c0=c0, c1=c1: e.tensor_tensor(out=stat[:, 58:59], in0=ssq[:, c0:c0 + 1], in1=ssq[:, c1:c1 + 1], op=ALU.add),
                     reads=[bf("ssqm")], writes=[bf("ssqs")])
                P.op("act", lambda e: e.activation(out=stat[:, 59:60], in_=stat[:, 58:59], func=AF.Ln, scale=1.0 / D, bias=stat[:, 63:64]),
                     reads=[bf("ssqs"), bf("epsb")], writes=[bf("lnv2")])
                P.op("act", lambda e: e.activation(out=stat[:, 60:61], in_=stat[:, 59:60], func=AF.Exp, scale=-0.5),
                     reads=[bf("lnv2")], writes=[bf("rstd2")])
                oslot = n % 2

                def post2(e, oslot=oslot, d0=d0):
                    e.scalar_tensor_tensor(out=ob[oslot][:, 0:512], in0=pf[d0][:], scalar=stat[:, 60:61], in1=g_post2[:, 0:512], op0=ALU.mult, op1=ALU.mult)
                    return e.scalar_tensor_tensor(out=ob[oslot][:, 512:1024], in0=pf[d0 + 1][:], scalar=stat[:, 60:61], in1=g_post2[:, 512:1024], op0=ALU.mult, op1=ALU.mult)
                P.op("dve", post2, reads=[B_pf[d0], B_pf[d0 + 1], bf("rstd2"), bf("g_post2")], writes=[bf("ob%d" % oslot)])
                P.op("dve", lambda e, oslot=oslot, c=c: e.tensor_tensor(out=ob[oslot], in0=ob[oslot], in1=xbt[:, c, :], op=ALU.add),
                     reads=[bf("ob%d" % oslot), bf("xbt%d" % c)], writes=[bf("ob%d" % oslot)])
                out_ops.append(P.op("sp", lambda e, n=n, oslot=oslot: [e.dma_start(out=out[n * 128:(n + 1) * 128, :], in_=ob[oslot])],
                                    reads=[bf("ob%d" % oslot)], writes=[bf("out%d" % n)], dsem=osem[oslot], ndma=1))
        if stop < 99:
            P.stopped = False
            s_dbg = dsem("s_dbg")
            out_ops.append(P.op("sp", lambda e: [e.dma_start(out=out[0:128, 0:512], in_=state), e.dma_start(out=out[128:256, :], in_=x1o[0]),
                                                e.dma_start(out=out[256:384, :], in_=x1o[1])],
                                reads=[bf("state"), bf("x1o0"), bf("x1o1")], dsem=s_dbg, ndma=3))
        P.op("sp", None, extra=[o for o in out_ops if getattr(o, 'dsem', None) is not None])

        P.finalize()

        block = es.enter_context(nc.Block())

        @block.sync
        def _(e):
            P.emit_engine("sp", e, esems)

        @block.scalar
        def _(e):
            P.emit_engine("act", e, esems)

        @block.vector
        def _(e):
            P.emit_engine("dve", e, esems)

        @block.gpsimd
        def _(e):
            P.emit_engine("pool", e, esems)

        @block.tensor
        def _(e):
            P.emit_engine("pe", e, esems)
    return nc


def _tables():
    half = np.linspace(0.0, 1.0, 64, dtype=np.float32)
    angle = (np.float32(1.0) / np.power(np.float32(10000.0), half)).astype(np.float32)
    angle = np.repeat(angle, 2)
    pos = np.arange(S, dtype=np.float32)
    arg = (pos[:, None] * angle[None]).astype(np.float32)
    sin = np.sin(arg).astype(np.float64)
    cos = np.cos(arg).astype(np.float64)
    sgn = np.where(np.arange(128) % 2 == 0, -1.0, 1.0)
    sinS = sin * sgn[None]
    i = np.arange(S) % 128
    g = np.array(GAMMA, dtype=np.float64)
    xi = g[None, :] ** (i[:, None] + 1.0)
    zeta = g[None, :] ** (127.0 - i[:, None])
    tq = np.empty((S, 2, 4, 128), np.float32)
    tk = np.empty((S, 2, 4, 128), np.float32)
    ks = 128.0 ** -0.5
    for h in range(4):
        tq[:, 0, h] = cos * xi[:, h:h + 1]
        tq[:, 1, h] = sinS * xi[:, h:h + 1]
        tk[:, 0, h] = cos * zeta[:, h:h + 1] * ks
        tk[:, 1, h] = sinS * zeta[:, h:h + 1] * ks
    return tq.reshape(S, 2, 512), tk.reshape(S, 2, 512)


_CACHE = {}


def kernel(x, mix_pre_norm, w_in, attn_sinks, w_out, mix_post_norm, ffn_pre_norm, w_up, conv_w, conv_b, w_down, ffn_post_norm):
    f32 = np.float32
    x = np.asarray(x, f32)[0]
    if "nc" not in _CACHE:
        import os
        _CACHE["nc"] = build_program(stop=float(os.environ.get("KSTOP", "99")))
        _CACHE["tabs"] = _tables()
    nc = _CACHE["nc"]
    tq, tk = _CACHE["tabs"]
    gains = np.stack([np.asarray(a, f32)[0] for a in (mix_pre_norm, mix_post_norm, ffn_pre_norm, ffn_post_norm)], 0)
    kk = np.arange(128)[:, None]
    qq = np.arange(128)[None, :]
    cur = (kk <= qq).astype(f32)
    prev = (kk > qq).astype(f32)
    rmask = np.concatenate([cur * f32(GAMMA[h] ** -128.0) for h in range(4)], axis=1).astype(f32)
    common = {
        "w_in": np.ascontiguousarray(np.asarray(w_in, f32)[0]),
        "w_out": np.ascontiguousarray(np.asarray(w_out, f32)[0]),
        "w_up": np.ascontiguousarray(np.asarray(w_up, f32)[0]),
        "w_down": np.ascontiguousarray(np.asarray(w_down, f32)[0]),
        "gains": np.ascontiguousarray(gains),
        "sinks": np.ascontiguousarray(np.asarray(attn_sinks, f32)),
        "conv_w": np.ascontiguousarray(np.asarray(conv_w, f32)[0].reshape(3, 44, 128).transpose(2, 0, 1).reshape(128, 132)),
        "conv_b": np.ascontiguousarray(np.asarray(conv_b, f32)[0].reshape(44, 128).T),
        "ident": np.eye(128, dtype=f32),
        "rmask": rmask,
    }
    in_maps = []
    for c in range(NCORES):
        t0 = c * TPC - (NPRE + 1) * 128
        xe = np.zeros((NX * 128, D), f32)
        tke = np.zeros((NX * 128, 2, 512), f32)
        lo = max(t0, 0)
        xe[lo - t0:] = x[lo:(c + 1) * TPC]
        tke[lo - t0:] = tk[lo:(c + 1) * TPC]
        tqe = np.zeros((NM * 128, 2, 512), f32)
        q0 = c * TPC - 128
        ql = max(q0, 0)
        tqe[ql - q0:] = tq[ql:(c + 1) * TPC]
        am = np.stack([np.tile(cur, (1, 4)), np.tile(prev, (1, 4)), np.tile(prev, (1, 4)) * (1.0 if c > 0 else 0.0)], axis=1).astype(f32)
        m = dict(common)
        m["x_ext"] = xe
        m["amask"] = np.ascontiguousarray(am)
        m["tq"] = tqe
        m["tk"] = tke
        m["coef"] = np.zeros((1, 40), f32)
        in_maps.append(m)
    import os
    if os.environ.get("KSAME"):
        in_maps = [in_maps[int(os.environ["KSAME"])]] * NCORES
    res = run_bass_kernel_spmd(nc, in_maps, core_ids=list(range(NCORES)))
    outs = [np.asarray(r["out"], f32) for r in res.results]
    return np.concatenate(outs, axis=0)[None]
```

```python
import math
from contextlib import ExitStack

import numpy as np
import concourse.bass as bass
import concourse.mybir as mybir
from concourse.bass_utils import run_bass_kernel_spmd

F32 = mybir.dt.float32
BF16 = mybir.dt.bfloat16
AF = mybir.ActivationFunctionType
ALU = mybir.AluOpType

NCORES = 8
S = 16384
D = 1024
TPC = S // NCORES
NCH = TPC // 128
IN_W = 2816
DFF = 2816
NFB = DFF // 128
EPS = 1e-6
GAMMA = [1.0 - 2.0 ** (-5.0 - h) for h in range(4)]
CDECAY = [g ** 128 for g in GAMMA]
NPRE = 44
NX = NPRE + NCH + 1
NM = NCH + 1


class Buf:
    __slots__ = ("name", "w", "r")

    def __init__(self, name):
        self.name = name
        self.w = None
        self.r = []


class DSem:
    def __init__(self, h):
        self.h = h
        self.count = 0


class Op:
    __slots__ = ("eng", "emit", "raw", "oth", "sig", "dsem", "dval", "needed", "dinc")

    def __init__(self):
        self.dsem = None


class Prog:
    ENG = ("sp", "act", "dve", "pool", "pe")

    def __init__(self):
        self.ops = {e: [] for e in self.ENG}
        self.dma_since_barrier = []
        self.stopped = False

    def op(self, eng, emit, reads=(), writes=(), dsem=None, ndma=0, extra=(), dinc=16):
        o = Op()
        if self.stopped:
            return o
        o.eng, o.emit, o.sig, o.needed = eng, emit, None, False
        o.dsem = dsem
        o.dval = None
        o.dinc = dinc
        if dsem is not None:
            dsem.count += dinc * ndma
            o.dval = dsem.count
            self.dma_since_barrier.append(o)
        raw, oth = [], []
        for b in reads:
            if b.w is not None:
                raw.append(b.w)
        for b in writes:
            if b.w is not None:
                oth.append(b.w)
            oth.extend(b.r)
        oth.extend(extra)
        o.raw = [d for d in dict.fromkeys(raw) if d is not o]
        o.oth = [d for d in dict.fromkeys(oth) if d is not o and d not in o.raw]
        for b in reads:
            b.r.append(o)
        for b in writes:
            b.w = o
            b.r = []
        self.ops[eng].append(o)
        return o

    def barrier(self):
        if self.stopped:
            return
        lasts = [self.ops[e][-1] for e in self.ENG if self.ops[e]]
        dmas = list(self.dma_since_barrier)
        self.dma_since_barrier = []
        for e in self.ENG:
            self.op(e, None, extra=[x for x in lasts + dmas])

    def finalize(self):
        for e in self.ENG:
            for o in self.ops[e]:
                for d in o.raw:
                    if d.dsem is None and (d.eng != o.eng or o.eng != "pe"):
                        d.needed = True
                for d in o.oth:
                    if d.dsem is None and d.eng != o.eng:
                        d.needed = True
        for e in self.ENG:
            n = 0
            for o in self.ops[e]:
                if o.needed:
                    n += 1
                    o.sig = n

    def emit_engine(self, e, engobj, esems):
        waited = {}

        def wait(sem, val):
            k = id(sem)
            if waited.get(k, 0) >= val:
                return
            waited[k] = val
            engobj.wait_ge(sem, val)

        for o in self.ops[e]:
            for d, israw in [(d, True) for d in o.raw] + [(d, False) for d in o.oth]:
                if d.dsem is not None:
                    wait(d.dsem.h, d.dval)
                elif d.eng == e:
                    if e != "pe" and israw:
                        wait(esems[e], d.sig)
                else:
                    wait(esems[d.eng], d.sig)
            if o.emit is None:
                continue
            r = o.emit(engobj)
            if o.dsem is not None:
                for ins in r:
                    if o.dinc == 1:
                        ins.then_inc(o.dsem.h)
                    else:
                        ins.then_inc(o.dsem.h, o.dinc)
            elif o.needed:
                r.then_inc(esems[e], 1)


def build_program(stop=99):
    nc = bass.Bass("TRN2", target_bir_lowering=False)
    P = Prog()

    def stage(k):
        if stop < k:
            P.stopped = True

    def din(name, shape):
        return nc.dram_tensor(name, shape, F32, kind="ExternalInput").ap()

    x_ext = din("x_ext", [NX * 128, D])
    w_in = din("w_in", [D, IN_W])
    w_out = din("w_out", [D, D])
    w_up = din("w_up", [D, 2 * DFF])
    w_down = din("w_down", [DFF, D])
    gains = din("gains", [4, D])
    sinks = din("sinks", [1, 8])
    conv_w = din("conv_w", [128, 3 * 44])
    conv_b = din("conv_b", [128, 44])
    ident = din("ident", [128, 128])
    amask_d = din("amask", [128, 3, 512])
    rmask_d = din("rmask", [128, 512])
    tq = din("tq", [NM * 128, 2, 512])
    tk = din("tk", [NX * 128, 2, 512])
    coef = din("coef", [1, 40])
    out = nc.dram_tensor("out", [TPC, D], F32, kind="ExternalOutput").ap()
    x1d = nc.dram_tensor("x1d", [TPC, D], F32)

    es = ExitStack()
    with es:
        def sb(name, shape, dt):
            return es.enter_context(nc.sbuf_tensor(name, shape, dt))

        def sem(name):
            return es.enter_context(nc.semaphore(name))

        def dsem(name):
            return DSem(sem(name))

        esems = {e: sem("e_" + e) for e in Prog.ENG}

        WA = 38912
        warena = sb("warena", [128, WA], BF16)
        aarena = sb("aarena", [128, 17408], BF16)
        FA = 14336
        farena = sb("farena", [128, FA], F32)

        def fa(off, n):
            return farena[:, off:off + n]

        identb = sb("identb", [128, 128], BF16)
        wkd = sb("wkd", [128, 8, 2, 128], BF16)
        hT = [sb("hT%d" % i, [128, 8, 128], BF16) for i in range(2)]
        xns = [sb("xn%d" % i, [128, 1024], BF16) for i in range(2)]
        xn = xns[0]
        qaT = sb("qaT", [128, 2, 512], BF16)
        kT = [sb("kT%d" % i, [128, 2, 128], BF16) for i in range(2)]
        vaug = [sb("vaug%d" % i, [128, 2, 128], BF16) for i in range(2)]
        ones64 = sb("ones64", [128, 128], BF16)
        PT = [sb("PT%d" % i, [128, 512], BF16) for i in range(4)]
        PTm = [sb("PTm%d" % i, [128, 512], BF16) for i in range(4)]
        kztmp = [PT[0], PT[1]]
        vrtmp = [PT[2], PT[3]]
        attnT = sb("attnT", [128, 2, 2, 128], BF16)
        qx = sb("qx", [128, 512], BF16)
        qxT = sb("qxT", [128, 4, 128], BF16)
        kzT = sb("kzT", [128, 4, 128], BF16)
        inT = sb("inT", [128, 512], BF16)
        stbf = sb("stbf", [128, 512], BF16)
        gated = sb("gated", [128, 512], BF16)
        retT = sb("retT", [128, 4, 128], BF16)
        amask = sb("amaskb", [128, 3, 512], BF16)
        ssq = sb("ssq", [128, 192], F32)
        xh = sb("xh", [2, 1024], F32)
        stat = sb("stat", [128, 128], F32)
        h2Th = sb("h2Th", [128, 8, 2], BF16)

        pb0 = es.enter_context(nc.psum_tensor("pb0", [128, 1024], BF16))
        pb1 = es.enter_context(nc.psum_tensor("pb1", [128, 1024], BF16))
        pf = [None, None] + [es.enter_context(nc.psum_tensor("pf%d" % i, [128, 512], F32)) for i in range(2, 8)]
        B_pb0, B_pb1 = Buf("pb0"), Buf("pb1")
        B_pf = [None, None] + [Buf("pf%d" % i) for i in range(2, 8)]

        winv = warena[:, 0:22528].rearrange("p (k n) -> p k n", k=8)
        woa = warena[:, 22528:26624].rearrange("p (h n) -> p h n", h=4)
        wor = warena[:, 26624:30720].rearrange("p (h n) -> p h n", h=4)
        wdn = warena[:, 0:22528].rearrange("p (j n) -> p j n", j=NFB)
        wupr = [warena[:, 22528 + i * 4096:22528 + (i + 1) * 4096].rearrange("p (k n) -> p k n", k=8) for i in range(4)]
        kzs = aarena[:, 0:8704].rearrange("p (c n) -> p c n", c=NM)
        vrs = aarena[:, 8704:17408].rearrange("p (c n) -> p c n", c=NM)
        yT = aarena[:, 0:11264].rearrange("p (j n) -> p j n", j=NFB)
        h2T = aarena[:, 11264:15360].rearrange("p (k n) -> p k n", k=8)

        B = {}

        def bf(name):
            if name not in B:
                B[name] = Buf(name)
            return B[name]

        ssq_col = [0]

        def newcol():
            ssq_col[0] += 1
            assert ssq_col[0] < 192
            return ssq_col[0] - 1

        GH = [0, 2, 1, 3]
        w_in_v = w_in.rearrange("(k p) n -> p k n", p=128)

        s_const = dsem("s_const")
        s_w = [dsem("s_w%d" % i) for i in range(6)]

        g_pre1, g_post1 = fa(0, 1024), fa(1024, 1024)
        xc = [fa(2048, 1024), fa(3072, 1024)]
        tab = [fa(4096, 1024).rearrange("p (a n) -> p a n", a=2), fa(5120, 1024).rearrange("p (a n) -> p a n", a=2)]
        Gst = fa(2048, 4096).rearrange("p (r n) -> p r n", r=8)
        t1, t2, sg, eg = fa(6144, 512), fa(6656, 512), fa(7168, 512), fa(7680, 512)
        x1o = [fa(8192, 1024), fa(9216, 1024)]
        esk = fa(10240, 1024).rearrange("p (j n) -> p j n", j=2)
        rmask = fa(11264, 512)
        state = fa(11776, 512)
        rn = fa(12288, 512)
        den = fa(12800, 512)
        tmpm = fa(13312, 1024)
        coefb = stat[:, 64:104]
        es8 = stat[:, 40:48]

        def c_loads(e):
            r = []
            r.append(e.dma_start(out=g_pre1, in_=gains[0:1, :].partition_broadcast(128)[:, 0, :]))
            r.append(e.dma_start(out=g_post1, in_=gains[1:2, :].partition_broadcast(128)[:, 0, :]))
            r.append(e.dma_start(out=rmask, in_=rmask_d[:, :]))
            r.append(e.dma_start(out=coefb, in_=coef.partition_broadcast(128)[:, 0, :]))
            r.append(e.dma_start(out=es8, in_=sinks.partition_broadcast(128)[:, 0, :]))
            return r
        P.op("sp", c_loads, writes=[bf("g_pre1"), bf("g_post1"), bf("rmask"), bf("coefb"), bf("es8")], dsem=s_const, ndma=5)

        def c_loads2(e):
            r = []
            r.append(e.dma_start(out=identb[:], in_=ident[:, :]))
            r.append(e.dma_start(out=amask[:], in_=amask_d[:, :, :]))
            return r
        s_const2 = dsem("s_const2")
        P.op("pool", c_loads2, writes=[bf("identb"), bf("amask")], dsem=s_const2, ndma=2)

        def w_a0(e):
            return [e.dma_start(out=winv[:, :, 1280:2304], in_=w_in_v[:, :, 1280:2304])]
        P.op("pool", w_a0, writes=[bf("win_kv")], dsem=s_w[0], ndma=1)

        def memsets(e):
            e.memset(ssq[:], 0.0)
            e.memset(state, 0.0)
            e.memset(qaT[:], 0.0)
            return e.memset(ones64[:], 1.0)
        P.op("dve", memsets, writes=[bf("ssq"), bf("state"), bf("ones64"), bf("qaT0")])

        def w_a1(e):
            r = [e.dma_start(out=winv[:, :, 0:1280], in_=w_in_v[:, :, 0:1280]),
                 e.dma_start(out=winv[:, :, 2304:2816], in_=w_in_v[:, :, 2304:2816])]
            for j in range(2):
                for dup in range(2):
                    r.append(e.dma_start(out=wkd[:, :, j, dup * 64:(dup + 1) * 64],
                                         in_=w_in_v[:, :, 512 + 64 * j:512 + 64 * (j + 1)]))
            for j in range(2):
                for a in range(2):
                    hl, hu = 4 * j + GH[2 * a], 4 * j + GH[2 * a + 1]
                    r.append(e.dma_start(out=woa[0:64, 2 * j + a, :], in_=w_out[64 * hl:64 * hl + 64, :]))
                    r.append(e.dma_start(out=woa[64:128, 2 * j + a, :], in_=w_out[64 * hu:64 * hu + 64, :]))
            r.append(e.dma_start(out=wor, in_=w_out[512:1024, :].rearrange("(h p) n -> p h n", p=128)))
            return r

        xsem = [dsem("xs0"), dsem("xs1")]
        tsem = [dsem("ts0"), dsem("ts1")]

        def norm_transpose(xrow0, slot, gain_ap, gain_buf, src_dram, load=True, xbuf=None, xap=None, dst=None, dstbuf=None, lsem=None, pbk=0):
            xa = xap if xap is not None else xc[slot]
            xb_ = xbuf if xbuf is not None else bf("xc%d" % slot)
            xs = slot % 2
            xn_ = xns[xs]
            xnb = bf("xn%d" % xs)
            cl, cr = (48, 49) if xs == 0 else (4, 5)
            lnb, rsb = bf("lnv%d" % xs), bf("rstd%d" % xs)
            pbt, pbb = (pb0, B_pb0) if pbk == 0 else (pb1, B_pb1)
            if load:
                P.op("sp", lambda e: [e.dma_start(out=xa, in_=src_dram[xrow0:xrow0 + 128, :])],
                     writes=[xb_], dsem=(lsem if lsem is not None else xsem[slot]), ndma=1)
            col = newcol()
            P.op("act", lambda e: e.activation(out=xn_[:], in_=xa, func=AF.Square, accum_out=ssq[:, col:col + 1]),
                 reads=[xb_, bf("ssq")], writes=[xnb, bf("ssqc%d" % xs)])
            P.op("act", lambda e: e.activation(out=stat[:, cl:cl + 1], in_=ssq[:, col:col + 1], func=AF.Ln, scale=1.0 / D, bias=stat[:, 63:64]),
                 reads=[bf("ssqc%d" % xs), bf("epsb")], writes=[lnb])
            P.op("act", lambda e: e.activation(out=stat[:, cr:cr + 1], in_=stat[:, cl:cl + 1], func=AF.Exp, scale=-0.5),
                 reads=[lnb], writes=[rsb])
            P.op("dve", lambda e: e.scalar_tensor_tensor(out=xn_[:], in0=xa, scalar=stat[:, cr:cr + 1], in1=gain_ap, op0=ALU.mult, op1=ALU.mult),
                 reads=[xb_, rsb, gain_buf], writes=[xnb])

            def tr(e):
                r = None
                for k in range(8):
                    r = e.transpose(out=pbt[:, k * 128:(k + 1) * 128], in_=xn_[:, k * 128:(k + 1) * 128], identity=identb[:])
                return r
            P.op("pe", tr, reads=[xnb, bf("identb")], writes=[pbb])
            d_ap = dst if dst is not None else hT[slot][:]
            d_buf = dstbuf if dstbuf is not None else bf("hT%d" % slot)
            P.op("act", lambda e: e.activation(out=d_ap, in_=pbt[:].rearrange("p (k n) -> p k n", k=8), func=AF.Copy),
                 reads=[pbb], writes=[d_buf])

        def cst(e):
            e.memset(stat[:, 62:63], 1.0)
            return e.memset(stat[:, 63:64], EPS)
        P.op("pool", cst, writes=[bf("epsb"), bf("oneb")])

        t1s, t2s = [t1, rn], [t2, den]
        t1b, t2b = [bf("t1"), bf("rn")], [bf("t2"), bf("den")]

        def rotate_ops(psbuf, psap, tslot, out_ap, out_buf, ts=0):
            tb = bf("tab%d" % tslot)
            t1_, t2_ = t1s[ts], t2s[ts]
            P.op("dve", lambda e: e.tensor_tensor(out=t1_, in0=psap, in1=tab[tslot][:, 0, :], op=ALU.mult),
                 reads=[psbuf, tb], writes=[t1b[ts]])

            def sw(e):
                pv = psap.rearrange("p (i two) -> p i two", two=2)
                sv = tab[tslot][:, 1, :].rearrange("p (i two) -> p i two", two=2)
                ov = t2_.rearrange("p (i two) -> p i two", two=2)
                e.tensor_tensor(out=ov[:, :, 0], in0=pv[:, :, 1], in1=sv[:, :, 0], op=ALU.mult)
                return e.tensor_tensor(out=ov[:, :, 1], in0=pv[:, :, 0], in1=sv[:, :, 1], op=ALU.mult)
            P.op("dve", sw, reads=[psbuf, tb], writes=[t2b[ts]])
            P.op("dve", lambda e: e.tensor_tensor(out=out_ap, in0=t1_, in1=t2_, op=ALU.add),
                 reads=[t1b[ts], t2b[ts]], writes=[out_buf])

        def kv_state_update(kap, vap, kbuf, vbuf, kb_=4):
            def kvmm(e):
                r = None
                for h in range(4):
                    r = e.matmul(pf[kb_][:, h * 128:(h + 1) * 128], lhsT=kap[:, h * 128:(h + 1) * 128],
                                 rhs=vap[:, h * 128:(h + 1) * 128], start=True, stop=True)
                return r
            P.op("pe", kvmm, reads=[kbuf, vbuf], writes=[B_pf[kb_]])

            def upd(e):
                r = None
                for h in range(4):
                    sl = slice(h * 128, (h + 1) * 128)
                    r = e.scalar_tensor_tensor(out=state[:, sl], in0=state[:, sl], scalar=float(CDECAY[h]), in1=pf[kb_][:, sl],
                                               op0=ALU.mult, op1=ALU.add)
                return r
            P.op("dve", upd, reads=[B_pf[kb_], bf("state")], writes=[bf("state")])

        stage(1)
        def lnrs(xs):
            return ((48, 49) if xs == 0 else (4, 5)), bf("lnv%d" % xs), bf("rstd%d" % xs)

        def pre_s1(i):
            slot = i % 2
            xa, xb_, xn_, xnb = xc[slot], bf("xc%d" % slot), xns[slot], bf("xn%d" % slot)
            (cl, cr), lnb, rsb = lnrs(slot)
            P.op("sp", lambda e: [e.dma_start(out=xa, in_=x_ext[128 * i:128 * i + 128, :])], writes=[xb_], dsem=xsem[slot], ndma=1)
            if i == 0:
                P.op("pool", w_a1, writes=[bf("win_rest"), bf("wkd"), bf("wout")], dsem=s_w[1], ndma=15)
            col = newcol()
            P.op("act", lambda e: e.activation(out=xn_[:], in_=xa, func=AF.Square, accum_out=ssq[:, col:col + 1]),
                 reads=[xb_, bf("ssq")], writes=[xnb, bf("ssqc%d" % slot)])
            P.op("act", lambda e: e.activation(out=stat[:, cl:cl + 1], in_=ssq[:, col:col + 1], func=AF.Ln, scale=1.0 / D, bias=stat[:, 63:64]),
                 reads=[bf("ssqc%d" % slot), bf("epsb")], writes=[lnb])
            P.op("act", lambda e: e.activation(out=stat[:, cr:cr + 1], in_=stat[:, cl:cl + 1], func=AF.Exp, scale=-0.5),
                 reads=[lnb], writes=[rsb])
            P.op("dve", lambda e: e.scalar_tensor_tensor(out=xn_[:], in0=xa, scalar=stat[:, cr:cr + 1], in1=g_pre1, op0=ALU.mult, op1=ALU.mult),
                 reads=[xb_, rsb, bf("g_pre1")], writes=[xnb])

        def pre_s2(i):
            slot = i % 2
            xn_, xnb = xns[slot], bf("xn%d" % slot)
            pbt, pbb = (pb0, B_pb0) if slot == 0 else (pb1, B_pb1)

            def tr(e):
                r = None
                for k in range(8):
                    r = e.transpose(out=pbt[:, k * 128:(k + 1) * 128], in_=xn_[:, k * 128:(k + 1) * 128], identity=identb[:])
                return r
            P.op("pe", tr, reads=[xnb, bf("identb")], writes=[pbb])
            P.op("act", lambda e: e.activation(out=hT[slot][:], in_=pbt[:].rearrange("p (k n) -> p k n", k=8), func=AF.Copy),
                 reads=[pbb], writes=[bf("hT%d" % slot)])

        def pre_kv(i):
            slot = i % 2
            if i < NPRE:
                return kztmp[slot][:], vrtmp[slot][:], bf("kztmp%d" % slot), bf("vrtmp%d" % slot)
            m = i - NPRE
            return kzs[:, m, :], vrs[:, m, :], bf("kz%d" % m), bf("vr%d" % m)

        def pre_s3(i):
            slot = i % 2
            bk, bv = (3, 5) if slot == 0 else (6, 7)

            def proj(e):
                r = None
                for (bank, c0) in ((bk, 1280), (bv, 1792)):
                    for k in range(8):
                        r = e.matmul(pf[bank][:], lhsT=hT[slot][:, k, :], rhs=winv[:, k, c0:c0 + 512], start=(k == 0), stop=(k == 7))
                return r
            P.op("pe", proj, reads=[bf("hT%d" % slot), bf("win_kv")], writes=[B_pf[bk], B_pf[bv]])
            kap, vap, kb, vb = pre_kv(i)
            P.op("act", lambda e: e.activation(out=vap, in_=pf[bv][:], func=AF.Copy), reads=[B_pf[bv]], writes=[vb])
            rotate_ops(B_pf[bk], pf[bk][:], slot, kap, kb, ts=slot)

        def pre_s4(i):
            if i >= NPRE:
                return
            kap, vap, kb, vb = pre_kv(i)
            kv_state_update(kap, vap, kb, vb, kb_=(4 if i % 2 == 0 else 2))

        for j in range(-2, NX + 1):
            if 0 <= j + 2 < NX:
                pre_s1(j + 2)
            if 0 <= j + 1 < NX:
                pre_s2(j + 1)
            if 0 <= j < NX:
                pre_s3(j)
            if 0 <= j - 1 < NX:
                pre_s4(j - 1)
            if 0 <= j + 2 < NX:
                P.op("pool", lambda e, i=j + 2: [e.dma_start(out=tab[i % 2], in_=tk[i * 128:(i + 1) * 128, :, :])],
                     writes=[bf("tab%d" % ((j + 2) % 2))], dsem=tsem[(j + 2) % 2], ndma=1)

        stage(2)
        stage(3)
        P.op("act", lambda e: e.activation(out=es8, in_=es8, func=AF.Exp), reads=[bf("es8")], writes=[bf("es8")])

        def esk_fill(e):
            r = None
            for j in range(2):
                for gs in range(4):
                    hq = 4 * j + GH[gs]
                    r = e.activation(out=esk[:, j, gs * 128:(gs + 1) * 128], in_=rmask[:, 0:128], func=AF.Identity, scale=0.0, bias=es8[:, hq:hq + 1])
            return r
        P.op("act", esk_fill, reads=[bf("es8"), bf("rmask")], writes=[bf("esk")])

        def kv_from_hT(slot_h, slot_kv, load_v=True):
            def mm(e):
                r = None
                for j in range(2):
                    for k in range(8):
                        r = e.matmul(pf[3][:, j * 128:(j + 1) * 128], lhsT=wkd[:, k, j, :], rhs=hT[slot_h][:, k, :], start=(k == 0), stop=(k == 7))
                for k in range(8):
                    r = e.matmul(pf[3][:, 256:384], lhsT=hT[slot_h][:, k, :], rhs=winv[:, k, 640:768], start=(k == 0), stop=(k == 7))
                return r
            P.op("pe", mm, reads=[bf("hT%d" % slot_h), bf("wkd"), bf("win_rest")], writes=[B_pf[3]])
            P.op("act", lambda e: e.activation(out=kT[slot_kv][:], in_=pf[3][:, 0:256].rearrange("p (j n) -> p j n", j=2), func=AF.Copy),
                 reads=[B_pf[3]], writes=[bf("kT%d" % slot_kv)])
            def vevac(e):
                e.activation(out=vaug[slot_kv][:, :, 0:64], in_=pf[3][:, 256:384].rearrange("p (j n) -> p j n", j=2), func=AF.Copy)
                return e.activation(out=vaug[slot_kv][:, :, 64:128], in_=pf[3][:, 256:384].rearrange("p (j n) -> p j n", j=2), func=AF.Copy)
            P.op("act", vevac, reads=[B_pf[3]], writes=[bf("va%d" % slot_kv)])

        norm_transpose(128 * (NPRE - 1), 0, g_pre1, bf("g_pre1"), x_ext)
        kv_from_hT(0, 1)

        def main_head(m):
            slot = m % 2
            P.op("pool", lambda e, m=m, slot=slot: [e.dma_start(out=tab[slot], in_=tq[m * 128:(m + 1) * 128, :, :])],
                 writes=[bf("tab%d" % slot)], dsem=tsem[slot], ndma=1)
            norm_transpose(128 * (NPRE + m), slot, g_pre1, bf("g_pre1"), x_ext)

        x1sem = [dsem("x1s0"), dsem("x1s1")]
        s_hb = dsem("s_hb")
        for m in range(NM):
            n = m - 1
            if m >= 1:
                stage(4 + m * 0.01)
            slot = m % 2
            cur, prv = m % 2, (m + 1) % 2
            if m == 0:
                main_head(0)

            def proj_q(e, slot=slot):
                r = None
                for pr in range(4):
                    for k in range(8):
                        r = e.matmul(pf[2][:, pr * 128:(pr + 1) * 128], lhsT=winv[:, k, pr * 128:(pr + 1) * 128], rhs=hT[slot][:, k, :],
                                     start=(k == 0), stop=(k == 7))
                return r
            P.op("pe", proj_q, reads=[bf("hT%d" % slot), bf("win_rest")], writes=[B_pf[2]])
            def qevac(e):
                e.activation(out=qaT[0:64, :, 0:256], in_=pf[2][0:64, :].rearrange("p (j n) -> p j n", j=2), func=AF.Copy)
                return e.activation(out=qaT[64:128, :, 256:512], in_=pf[2][64:128, :].rearrange("p (j n) -> p j n", j=2), func=AF.Copy)
            P.op("act", qevac, reads=[B_pf[2], bf("qaT0")], writes=[bf("qaT")])
            kv_from_hT(slot, cur)

            def proj_r(e, slot=slot):
                r = None
                for (bank, c0) in ((4, 768), (5, 2304)):
                    for k in range(8):
                        r = e.matmul(pf[bank][:], lhsT=hT[slot][:, k, :], rhs=winv[:, k, c0:c0 + 512], start=(k == 0), stop=(k == 7))
                return r
            P.op("pe", proj_r, reads=[bf("hT%d" % slot), bf("win_rest")], writes=[B_pf[4], B_pf[5]])
            rotate_ops(B_pf[4], pf[4][:], slot, qx[:], bf("qx"))
            P.op("act", lambda e: e.activation(out=eg, in_=pf[5][:], func=AF.Exp, scale=-1.0), reads=[B_pf[5]], writes=[bf("eg")])
            P.op("act", lambda e: e.activation(out=eg, in_=eg, func=AF.Ln, scale=1.0, bias=stat[:, 62:63]), reads=[bf("eg"), bf("oneb")], writes=[bf("eg")])
            P.op("act", lambda e: e.activation(out=eg, in_=eg, func=AF.Exp, scale=-1.0), reads=[bf("eg")], writes=[bf("eg")])
            P.op("dve", lambda e: e.tensor_tensor(out=sg, in0=eg, in1=pf[5][:], op=ALU.mult), reads=[bf("eg"), B_pf[5]], writes=[bf("sg")])

            stage(3.2 if m == 0 else 4 + m * 0.01 + 0.002)
            if m + 1 < NM:
                main_head(m + 1)
            first = (m == 1)
            for j in range(2):
                for bi, (kslot, mi) in enumerate(((prv, 2 if first else 1), (cur, 0))):
                    ti = j * 2 + bi
                    bank = 6 + (ti % 2)

                    def sc(e, j=j, kslot=kslot, bank=bank):
                        return e.matmul(pf[bank][:], lhsT=kT[kslot][:, j, :], rhs=qaT[:, j, :], start=True, stop=True)
                    P.op("pe", sc, reads=[bf("kT%d" % kslot), bf("qaT")], writes=[B_pf[bank]])
                    P.op("act", lambda e, ti=ti, bank=bank: e.activation(out=PT[ti][:], in_=pf[bank][:], func=AF.Exp, scale=0.125),
                         reads=[B_pf[bank]], writes=[bf("PT%d" % ti)])
                    P.op("dve", lambda e, ti=ti, mi=mi: e.tensor_tensor(out=PTm[ti][:], in0=PT[ti][:], in1=amask[:, mi, :], op=ALU.mult),
                         reads=[bf("PT%d" % ti), bf("amask")], writes=[bf("PTm%d" % ti)])
            for j in range(2):
                def pv(e, j=j, prv=prv, cur=cur):
                    e.matmul(pf[2][:], lhsT=vaug[prv][:, j, :], rhs=PTm[2 * j][:], start=True, stop=False)
                    e.matmul(pf[2][:], lhsT=vaug[cur][:, j, :], rhs=PTm[2 * j + 1][:], start=False, stop=True)
                    e.matmul(pf[3][:], lhsT=ones64[:], rhs=PTm[2 * j][:], start=True, stop=False)
                    return e.matmul(pf[3][:], lhsT=ones64[:], rhs=PTm[2 * j + 1][:], start=False, stop=True)
                P.op("pe", pv, reads=[bf("va%d" % prv), bf("va%d" % cur), bf("PTm%d" % (2 * j)), bf("PTm%d" % (2 * j + 1)), bf("ones64")],
                     writes=[B_pf[2], B_pf[3]])
                def lnden(e, j=j):
                    r = None
                    for gs in range(4):
                        hq = 4 * j + GH[gs]
                        r = e.activation(out=den[:, gs * 128:(gs + 1) * 128], in_=pf[3][:, gs * 128:(gs + 1) * 128], func=AF.Ln,
                                         scale=1.0, bias=es8[:, hq:hq + 1])
                    return r
                P.op("act", lnden, reads=[B_pf[3], bf("es8")], writes=[bf("den")])
                P.op("act", lambda e: e.activation(out=den, in_=den, func=AF.Exp, scale=-1.0), reads=[bf("den")], writes=[bf("den")])

                def anorm(e, j=j):
                    ov = pf[2][:].rearrange("p (a b q) -> p a b q", a=2, b=2)
                    dv = den.rearrange("p (a b q) -> p a b q", a=2, b=2)
                    e.tensor_tensor(out=attnT[0:64, j, :, :], in0=ov[0:64, :, 0, :], in1=dv[0:64, :, 0, :], op=ALU.mult)
                    return e.tensor_tensor(out=attnT[64:128, j, :, :], in0=ov[64:128, :, 1, :], in1=dv[64:128, :, 1, :], op=ALU.mult)
                P.op("dve", anorm, reads=[B_pf[2], bf("den")], writes=[bf("attnT%d" % j)])

            stage(3.4 if m == 0 else 4 + m * 0.01 + 0.004)
            def trqk(e, n=m):
                r = None
                for h in range(4):
                    r = e.transpose(out=pb1[:, h * 128:(h + 1) * 128], in_=qx[:, h * 128:(h + 1) * 128], identity=identb[:])
                for h in range(4):
                    r = e.transpose(out=pb1[:, 512 + h * 128:512 + (h + 1) * 128], in_=kzs[:, n, h * 128:(h + 1) * 128], identity=identb[:])
                return r
            P.op("pe", trqk, reads=[bf("qx"), bf("kz%d" % m), bf("identb")], writes=[B_pb1])
            P.op("act", lambda e: e.activation(out=qxT[:], in_=pb1[:, 0:512].rearrange("p (h n) -> p h n", h=4), func=AF.Copy),
                 reads=[B_pb1], writes=[bf("qxT")])
            P.op("act", lambda e: e.activation(out=kzT[:], in_=pb1[:, 512:1024].rearrange("p (h n) -> p h n", h=4), func=AF.Copy),
                 reads=[B_pb1], writes=[bf("kzT")])

            def inner(e):
                r = None
                for h in range(4):
                    r = e.matmul(pf[4][:, h * 128:(h + 1) * 128], lhsT=kzT[:, h, :], rhs=qxT[:, h, :], start=True, stop=True)
                return r
            P.op("pe", inner, reads=[bf("kzT"), bf("qxT")], writes=[B_pf[4]])
            P.op("dve", lambda e: e.tensor_tensor(out=inT[:], in0=pf[4][:], in1=rmask, op=ALU.mult),
                 reads=[B_pf[4], bf("rmask")], writes=[bf("inT")])
            P.op("pool", lambda e: e.tensor_copy(out=stbf[:], in_=state), reads=[bf("state")], writes=[bf("stbf")])

            def omm(e, n=m):
                r = None
                for h in range(4):
                    sl = slice(h * 128, (h + 1) * 128)
                    e.matmul(pf[6][:, sl], lhsT=inT[:, sl], rhs=vrs[:, n, sl], start=True, stop=False)
                    r = e.matmul(pf[6][:, sl], lhsT=qxT[:, h, :], rhs=stbf[:, sl], start=False, stop=True)
                return r
            P.op("pe", omm, reads=[bf("inT"), bf("vr%d" % m), bf("qxT"), bf("stbf")], writes=[B_pf[6]])
            if m < NM - 1:
                kv_state_update(kzs[:, m, :], vrs[:, m, :], bf("kz%d" % m), bf("vr%d" % m))

            def gn_stats(e):
                r = None
                for h in range(4):
                    r = e.bn_stats(out=stat[:, 8 + 6 * h:14 + 6 * h], in_=pf[6][:, h * 128:(h + 1) * 128])
                return r
            P.op("dve", gn_stats, reads=[B_pf[6]], writes=[bf("bnst")])

            def gn_aggr(e):
                r = None
                for h in range(4):
                    r = e.bn_aggr(out=stat[:, 32 + 2 * h:34 + 2 * h], in_=stat[:, 8 + 6 * h:14 + 6 * h])
                return r
            P.op("dve", gn_aggr, reads=[bf("bnst")], writes=[bf("mv")])
            mvv = stat[:, 32:40].rearrange("p (h two) -> p h two", two=2)
            P.op("act", lambda e: e.activation(out=stat[:, 0:4], in_=mvv[:, :, 1], func=AF.Ln, scale=1.0, bias=stat[:, 63:64]),
                 reads=[bf("mv"), bf("epsb")], writes=[bf("glnv")])
            P.op("act", lambda e: e.activation(out=stat[:, 50:54], in_=stat[:, 0:4], func=AF.Exp, scale=-0.5),
                 reads=[bf("glnv")], writes=[bf("grstd")])
            P.op("dve", lambda e: e.scalar_tensor_tensor(out=stat[:, 54:58], in0=mvv[:, :, 0], scalar=-1.0, in1=stat[:, 50:54], op0=ALU.mult, op1=ALU.mult),
                 reads=[bf("mv"), bf("grstd")], writes=[bf("gnmr")])

            def gn_apply(e):
                r = None
                for h in range(4):
                    sl = slice(h * 128, (h + 1) * 128)
                    r = e.activation(out=rn[:, sl], in_=pf[6][:, sl], func=AF.Identity, scale=stat[:, 50 + h:51 + h], bias=stat[:, 54 + h:55 + h])
                return r
            P.op("act", gn_apply, reads=[B_pf[6], bf("grstd"), bf("gnmr")], writes=[bf("rn")])
            P.op("dve", lambda e: e.tensor_tensor(out=gated[:], in0=rn, in1=sg, op=ALU.mult), reads=[bf("rn"), bf("sg")], writes=[bf("gated")])

            def trr(e):
                r = None
                for h in range(4):
                    r = e.transpose(out=pb1[:, h * 128:(h + 1) * 128], in_=gated[:, h * 128:(h + 1) * 128], identity=identb[:])
                return r
            P.op("pe", trr, reads=[bf("gated"), bf("identb")], writes=[B_pb1])
            P.op("act", lambda e: e.activation(out=retT[:], in_=pb1[:, 0:512].rearrange("p (h n) -> p h n", h=4), func=AF.Copy),
                 reads=[B_pb1], writes=[bf("retT")])

            stage(3.6 if m == 0 else 4 + m * 0.01 + 0.006)
            def wout(e):
                r = None
                for cb in range(2):
                    bank = 6 + cb
                    cs = slice(cb * 512, (cb + 1) * 512)
                    first_mm = True
                    for j in range(2):
                        for a in range(2):
                            e.matmul(pf[bank][:], lhsT=attnT[:, j, a, :], rhs=woa[:, 2 * j + a, cs], start=first_mm, stop=False)
                            first_mm = False
                    for h in range(4):
                        r = e.matmul(pf[bank][:], lhsT=retT[:, h, :], rhs=wor[:, h, cs], start=False, stop=(h == 3))
                return r
            P.op("pe", wout, reads=[bf("attnT0"), bf("attnT1"), bf("retT"), bf("wout")], writes=[B_pf[6], B_pf[7]])

            stage(3.8 if m == 0 else 4 + m * 0.01 + 0.008)
            c0, c1 = newcol(), newcol()

            def sq2(e, c0=c0, c1=c1):
                e.activation(out=PT[0][:], in_=pf[6][:], func=AF.Square, accum_out=ssq[:, c0:c0 + 1])
                return e.activation(out=PT[1][:], in_=pf[7][:], func=AF.Square, accum_out=ssq[:, c1:c1 + 1])
            P.op("act", sq2, reads=[B_pf[6], B_pf[7], bf("ssq")], writes=[bf("PT0"), bf("PT1"), bf("ssqm")])
            P.op("dve", lambda e, c0=c0, c1=c1: e.tensor_tensor(out=stat[:, 58:59], in0=ssq[:, c0:c0 + 1], in1=ssq[:, c1:c1 + 1], op=ALU.add),
                 reads=[bf("ssqm")], writes=[bf("ssqs")])
            P.op("act", lambda e: e.activation(out=stat[:, 59:60], in_=stat[:, 58:59], func=AF.Ln, scale=1.0 / D, bias=stat[:, 63:64]),
                 reads=[bf("ssqs"), bf("epsb")], writes=[bf("lnv2")])
            P.op("act", lambda e: e.activation(out=stat[:, 60:61], in_=stat[:, 59:60], func=AF.Exp, scale=-0.5),
                 reads=[bf("lnv2")], writes=[bf("rstd2")])

            def post(e):
                e.scalar_tensor_tensor(out=tmpm[:, 0:512], in0=pf[6][:], scalar=stat[:, 60:61], in1=g_post1[:, 0:512], op0=ALU.mult, op1=ALU.mult)
                return e.scalar_tensor_tensor(out=tmpm[:, 512:1024], in0=pf[7][:], scalar=stat[:, 60:61], in1=g_post1[:, 512:1024], op0=ALU.mult, op1=ALU.mult)
            P.op("dve", post, reads=[B_pf[6], B_pf[7], bf("rstd2"), bf("g_post1")], writes=[bf("tmpm")])
            oslot = m % 2
            P.op("dve", lambda e, slot=slot, oslot=oslot: e.tensor_tensor(out=x1o[oslot], in0=tmpm, in1=xc[slot], op=ALU.add),
                 reads=[bf("tmpm"), bf("xc%d" % slot)], writes=[bf("x1o%d" % oslot)])
            if m == 0:
                P.op("sp", lambda e, oslot=oslot: [e.dma_start(out=xh[:], in_=x1o[oslot][126:128, :])],
                     reads=[bf("x1o%d" % oslot)], writes=[bf("xh")], dsem=s_hb, ndma=1)
            else:
                P.op("sp", lambda e, n=n, oslot=oslot: [e.dma_start(out=x1d[n * 128:(n + 1) * 128, :], in_=x1o[oslot])],
                     reads=[bf("x1o%d" % oslot)], writes=[bf("x1d%d" % n)], dsem=x1sem[oslot], ndma=1)

        stage(5)
        P.barrier()
        stage(6)

        g_pre2, g_post2 = fa(0, 1024), fa(1024, 1024)
        xbt = fa(2048, 4096).rearrange("p (c n) -> p c n", c=4)
        U = [fa(6144 + i * 516, 516) for i in range(4)]
        acc = [fa(8208 + i * 512, 512) for i in range(4)]
        gl = [fa(10256 + i * 512, 512) for i in range(2)]
        ob = [fa(11280 + i * 1024, 1024) for i in range(2)]
        carry = fa(13328, 88).rearrange("p (j t) -> p j t", t=2)
        cw = fa(13416, 132).rearrange("p (k j) -> p k j", k=3)
        cbv = fa(13548, 44)

        s_b = dsem("s_b")

        def b_loads(e):
            r = []
            r.append(e.dma_start(out=g_pre2, in_=gains[2:3, :].partition_broadcast(128)[:, 0, :]))
            r.append(e.dma_start(out=g_post2, in_=gains[3:4, :].partition_broadcast(128)[:, 0, :]))
            r.append(e.dma_start(out=fa(13416, 132), in_=conv_w[:, :]))
            r.append(e.dma_start(out=cbv, in_=conv_b[:, :]))
            return r
        P.op("sp", b_loads, writes=[bf("g_pre2"), bf("g_post2"), bf("cw"), bf("cbv")], dsem=s_b, ndma=4)
        s_wd = dsem("s_wd")
        w_down_v = w_down.rearrange("(j p) n -> p j n", p=128)
        w_up_v = w_up.rearrange("(k p) n -> p k n", p=128)

        NPC = 6
        pw = [512] * 5 + [256]
        ring_sem = [dsem("s_up%d" % i) for i in range(4)]
        ring_state = {"i": 0}

        def load_piece(part, q):
            i = ring_state["i"] % 4
            ring_state["i"] += 1
            c0 = part * DFF + q * 512
            w = pw[q]
            P.op("pool", lambda e, i=i, c0=c0, w=w: [e.dma_start(out=wupr[i][:, :, 0:w], in_=w_up_v[:, :, c0:c0 + w])],
                 writes=[bf("wupr%d" % i)], dsem=ring_sem[i], ndma=1)
            return i

        seq = [(q, part) for q in range(NPC) for part in (0, 1)]

        colh = newcol()
        P.op("act", lambda e: e.activation(out=xns[1][0:2, :], in_=xh[:], func=AF.Square, accum_out=ssq[0:2, colh:colh + 1]),
             reads=[bf("xh"), bf("ssq")], writes=[bf("xn1"), bf("ssqc")])
        P.op("act", lambda e: e.activation(out=stat[0:2, 48:49], in_=ssq[0:2, colh:colh + 1], func=AF.Ln, scale=1.0 / D, bias=stat[0:2, 63:64]),
             reads=[bf("ssqc"), bf("epsb")], writes=[bf("lnv")])
        P.op("act", lambda e: e.activation(out=stat[0:2, 49:50], in_=stat[0:2, 48:49], func=AF.Exp, scale=-0.5),
             reads=[bf("lnv")], writes=[bf("rstd")])
        P.op("dve", lambda e: e.memset(xn[:], 0.0), writes=[bf("xn0")])
        P.op("dve", lambda e: e.scalar_tensor_tensor(out=xn[0:2, :], in0=xh[:], scalar=stat[0:2, 49:50], in1=g_pre2[0:2, :], op0=ALU.mult, op1=ALU.mult),
             reads=[bf("xh"), bf("rstd"), bf("g_pre2"), bf("xn0")], writes=[bf("xn0")])

        def trh(e):
            r = None
            for k in range(8):
                r = e.transpose(out=pb0[:, k * 128:(k + 1) * 128], in_=xn[:, k * 128:(k + 1) * 128], identity=identb[:])
            return r
        P.op("pe", trh, reads=[bf("xn0"), bf("identb")], writes=[B_pb0])
        P.op("act", lambda e: e.activation(out=h2Th[:], in_=pb0[:].rearrange("p (k n) -> p k n", k=8)[:, :, 0:2], func=AF.Copy),
             reads=[B_pb0], writes=[bf("h2Th")])

        xbsem = [dsem("xb%d" % i) for i in range(4)]

        def x1_load(t, c):
            n = 4 * t + c
            P.op("sp", lambda e: [e.dma_start(out=xbt[:, c, :], in_=x1d[n * 128:(n + 1) * 128, :])],
                 writes=[bf("xbt%d" % c)], dsem=xbsem[c], ndma=1)
        osem = [dsem("os0"), dsem("os1")]
        out_ops = []
        NT = NCH // 4
        for t in range(NT):
            if t == 1:
                stage(7)
            if t == 0:
                for c in range(4):
                    x1_load(0, c)
            for c in range(4):
                n = 4 * t + c
                norm_transpose(128 * n, c % 2, g_pre2, bf("g_pre2"), x1d.ap(), load=False, xbuf=bf("xbt%d" % c), xap=xbt[:, c, :],
                               dst=h2T[:, :, c * 128:(c + 1) * 128], dstbuf=bf("h2T%d" % c), lsem=xbsem[c], pbk=c % 2)
            if t == 0:
                P.op("pool", lambda e: [e.dma_start(out=wdn[:, a:b, :], in_=w_down_v[:, a:b, :]) for (a, b) in ((0, 6), (6, 12), (12, 18), (18, 22))],
                     writes=[bf("wdn")], dsem=s_wd, ndma=4)
            pieces = {}
            pending = list(seq)
            for _ in range(3):
                q, part = pending.pop(0)
                pieces[(q, part)] = load_piece(part, q)
            ui = 0
            deferred = []
            for q in range(NPC):
                nb = pw[q] // 128
                for _ in range(1 if q == 0 else 2):
                    if pending:
                        q2, p2 = pending.pop(0)
                        pieces[(q2, p2)] = load_piece(p2, q2)
                for jb in range(nb):
                    j = q * 4 + jb
                    accs = []
                    taps = []
                    for part in (0, 1):
                        ri = pieces[(q, part)]
                        jj = part * NFB + j
                        bank = 2 + (ui % 4)
                        us = ui % 4
                        ui += 1
                        wsl = slice(jb * 128, (jb + 1) * 128)
                        if t == 0:
                            def cmm(e, ri=ri, wsl=wsl):
                                r = None
                                for k in range(8):
                                    r = e.matmul(pf[6][:, 0:2], lhsT=wupr[ri][:, k, wsl], rhs=h2Th[:, k, :], start=(k == 0), stop=(k == 7))
                                return r
                            P.op("pe", cmm, reads=[bf("wupr%d" % ri), bf("h2Th")], writes=[B_pf[6]])
                            P.op("act", lambda e, jj=jj: e.activation(out=carry[:, jj, :], in_=pf[6][:, 0:2], func=AF.Copy),
                                 reads=[B_pf[6]], writes=[bf("carry%d" % jj)])

                        def umm(e, ri=ri, wsl=wsl, bank=bank):
                            r = None
                            for k in range(8):
                                r = e.matmul(pf[bank][:], lhsT=wupr[ri][:, k, wsl], rhs=h2T[:, k, :], start=(k == 0), stop=(k == 7))
                            return r
                        P.op("pe", umm, reads=[bf("wupr%d" % ri)] + [bf("h2T%d" % c) for c in range(4)], writes=[B_pf[bank]])
                        P.op("pool", lambda e, us=us, jj=jj: e.tensor_copy(out=U[us][:, 0:2], in_=carry[:, jj, :]),
                             reads=[bf("carry%d" % jj)], writes=[bf("Uh%d" % us)])
                        P.op("act", lambda e, us=us, bank=bank: e.activation(out=U[us][:, 2:514], in_=pf[bank][:], func=AF.Copy),
                             reads=[B_pf[bank]], writes=[bf("U%d" % us)])
                        P.op("pool", lambda e, us=us, jj=jj: e.tensor_copy(out=carry[:, jj, :], in_=U[us][:, 512:514]),
                             reads=[bf("U%d" % us)], writes=[bf("carry%d" % jj)])
                        P.op("act", lambda e, us=us, jj=jj, bank=bank: e.activation(out=acc[us], in_=pf[bank][:], func=AF.Identity,
                                                                                   scale=cw[:, 2, jj:jj + 1], bias=cbv[:, jj:jj + 1]),
                             reads=[B_pf[bank], bf("cw"), bf("cbv")], writes=[bf("acc%d" % us)])
                        taps.append((us, jj))
                        accs.append(us)
                    for tapk, (c_lo, c_hi) in ((1, (1, 513)), (0, (0, 512))):
                        for (us, jj) in taps:
                            P.op("dve", lambda e, us=us, jj=jj, tapk=tapk, c_lo=c_lo, c_hi=c_hi: e.scalar_tensor_tensor(
                                out=acc[us], in0=U[us][:, c_lo:c_hi], scalar=cw[:, tapk, jj:jj + 1], in1=acc[us], op0=ALU.mult, op1=ALU.add),
                                 reads=[bf("U%d" % us), bf("Uh%d" % us), bf("cw"), bf("acc%d" % us)], writes=[bf("acc%d" % us)])

                    def fin(j=j, accs=tuple(accs)):
                        gi = j % 2
                        P.op("act", lambda e, gi=gi, a0=accs[0]: e.activation(out=gl[gi], in_=acc[a0], func=AF.Gelu_apprx_tanh),
                             reads=[bf("acc%d" % accs[0])], writes=[bf("gl%d" % gi)])
                        P.op("dve", lambda e, gi=gi, a1=accs[1], j=j: e.tensor_tensor(out=yT[:, j, :], in0=gl[gi], in1=acc[a1], op=ALU.mult),
                             reads=[bf("gl%d" % gi), bf("acc%d" % accs[1])], writes=[bf("yT%d" % j)])
                    if deferred:
                        deferred.pop()()
                    deferred.append(fin)
            while deferred:
                deferred.pop()()
            for c in range(4):
                n = 4 * t + c

                d0 = 6 if c % 2 == 0 else 4

                def dmm(e, c=c, d0=d0):
                    r = None
                    for cb in range(2):
                        for j in range(NFB):
                            r = e.matmul(pf[d0 + cb][:], lhsT=yT[:, j, c * 128:(c + 1) * 128], rhs=wdn[:, j, cb * 512:(cb + 1) * 512],
                                         start=(j == 0), stop=(j == NFB - 1))
                    return r
                P.op("pe", dmm, reads=[bf("yT%d" % j) for j in range(NFB)] + [bf("wdn")], writes=[B_pf[d0], B_pf[d0 + 1]])
                c0, c1 = newcol(), newcol()

                def sq3(e, c0=c0, c1=c1, d0=d0):
                    e.activation(out=PT[0][:], in_=pf[d0][:], func=AF.Square, accum_out=ssq[:, c0:c0 + 1])
                    return e.activation(out=PT[1][:], in_=pf[d0 + 1][:], func=AF.Square, accum_out=ssq[:, c1:c1 + 1])
                P.op("act", sq3, reads=[B_pf[d0], B_pf[d0 + 1], bf("ssq")], writes=[bf("PT0"), bf("PT1"), bf("ssqm")])
                P.op("dve", lambda e, c0=c0, c1=c1: e.tensor_tensor(out=stat[:, 58:59], in0=ssq[:, c0:c0 + 1], in1=ssq[:, c1:c1 + 1], op=ALU.add),
                     reads=[bf("ssqm")], writes=[bf("ssqs")])
                P.op("act", lambda e: e.activation(out=stat[:, 59:60], in_=stat[:, 58:59], func=AF.Ln, scale=1.0 / D, bias=stat[:, 63:64]),
                     reads=[bf("ssqs"), bf("epsb")], writes=[bf("lnv2")])
                P.op("act", lambda e: e.activation(out=stat[:, 60:61], in_=stat[:, 59:60], func=AF.Exp, scale=-0.5),
                     reads=[bf("lnv2")], writes=[bf("rstd2")])
                oslot = n % 2

                def post2(e, oslot=oslot, d0=d0):
                    e.scalar_tensor_tensor(out=ob[oslot][:, 0:512], in0=pf[d0][:], scalar=stat[:, 60:61], in1=g_post2[:, 0:512], op0=ALU.mult, op1=ALU.mult)
                    return e.scalar_tensor_tensor(out=ob[oslot][:, 512:1024], in0=pf[d0 + 1][:], scalar=stat[:, 60:61], in1=g_post2[:, 512:1024], op0=ALU.mult, op1=ALU.mult)
                P.op("dve", post2, reads=[B_pf[d0], B_pf[d0 + 1], bf("rstd2"), bf("g_post2")], writes=[bf("ob%d" % oslot)])
                P.op("dve", lambda e, oslot=oslot, c=c: e.tensor_tensor(out=ob[oslot], in0=ob[oslot], in1=xbt[:, c, :], op=ALU.add),
                     reads=[bf("ob%d" % oslot), bf("xbt%d" % c)], writes=[bf("ob%d" % oslot)])
                out_ops.append(P.op("sp", lambda e, n=n, oslot=oslot: [e.dma_start(out=out[n * 128:(n + 1) * 128, :], in_=ob[oslot])],
                                    reads=[bf("ob%d" % oslot)], writes=[bf("out%d" % n)], dsem=osem[oslot], ndma=1))
                if t + 1 < NT:
                    x1_load(t + 1, c)
        if stop < 99:
            P.stopped = False
            s_dbg = dsem("s_dbg")
            out_ops.append(P.op("sp", lambda e: [e.dma_start(out=out[0:128, 0:512], in_=state), e.dma_start(out=out[128:256, :], in_=x1o[0]),
                                                e.dma_start(out=out[256:384, :], in_=x1o[1])],
                                reads=[bf("state"), bf("x1o0"), bf("x1o1")], dsem=s_dbg, ndma=3))
        P.op("sp", None, extra=[o for o in out_ops if getattr(o, 'dsem', None) is not None])

        P.finalize()

        block = es.enter_context(nc.Block())

        @block.sync
        def _(e):
            P.emit_engine("sp", e, esems)

        @block.scalar
        def _(e):
            P.emit_engine("act", e, esems)

        @block.vector
        def _(e):
            P.emit_engine("dve", e, esems)

        @block.gpsimd
        def _(e):
            P.emit_engine("pool", e, esems)

        @block.tensor
        def _(e):
            P.emit_engine("pe", e, esems)
    return nc


def _tables():
    half = np.linspace(0.0, 1.0, 64, dtype=np.float32)
    angle = (np.float32(1.0) / np.power(np.float32(10000.0), half)).astype(np.float32)
    angle = np.repeat(angle, 2)
    pos = np.arange(S, dtype=np.float32)
    arg = (pos[:, None] * angle[None]).astype(np.float32)
    sin = np.sin(arg).astype(np.float64)
    cos = np.cos(arg).astype(np.float64)
    sgn = np.where(np.arange(128) % 2 == 0, -1.0, 1.0)
    sinS = sin * sgn[None]
    i = np.arange(S) % 128
    g = np.array(GAMMA, dtype=np.float64)
    xi = g[None, :] ** (i[:, None] + 1.0)
    zeta = g[None, :] ** (127.0 - i[:, None])
    tq = np.empty((S, 2, 4, 128), np.float32)
    tk = np.empty((S, 2, 4, 128), np.float32)
    ks = 128.0 ** -0.5
    for h in range(4):
        tq[:, 0, h] = cos * xi[:, h:h + 1]
        tq[:, 1, h] = sinS * xi[:, h:h + 1]
        tk[:, 0, h] = cos * zeta[:, h:h + 1] * ks
        tk[:, 1, h] = sinS * zeta[:, h:h + 1] * ks
    return tq.reshape(S, 2, 512), tk.reshape(S, 2, 512)


_CACHE = {}


def kernel(x, mix_pre_norm, w_in, attn_sinks, w_out, mix_post_norm, ffn_pre_norm, w_up, conv_w, conv_b, w_down, ffn_post_norm):
    f32 = np.float32
    x = np.asarray(x, f32)[0]
    if "nc" not in _CACHE:
        import os
        _CACHE["nc"] = build_program(stop=float(os.environ.get("KSTOP", "99")))
        _CACHE["tabs"] = _tables()
    nc = _CACHE["nc"]
    tq, tk = _CACHE["tabs"]
    gains = np.stack([np.asarray(a, f32)[0] for a in (mix_pre_norm, mix_post_norm, ffn_pre_norm, ffn_post_norm)], 0)
    kk = np.arange(128)[:, None]
    qq = np.arange(128)[None, :]
    cur = (kk <= qq).astype(f32)
    prev = (kk > qq).astype(f32)
    rmask = np.concatenate([cur * f32(GAMMA[h] ** -128.0) for h in range(4)], axis=1).astype(f32)
    common = {
        "w_in": np.ascontiguousarray(np.asarray(w_in, f32)[0]),
        "w_out": np.ascontiguousarray(np.asarray(w_out, f32)[0]),
        "w_up": np.ascontiguousarray(np.asarray(w_up, f32)[0]),
        "w_down": np.ascontiguousarray(np.asarray(w_down, f32)[0]),
        "gains": np.ascontiguousarray(gains),
        "sinks": np.ascontiguousarray(np.asarray(attn_sinks, f32)),
        "conv_w": np.ascontiguousarray(np.asarray(conv_w, f32)[0].reshape(3, 44, 128).transpose(2, 0, 1).reshape(128, 132)),
        "conv_b": np.ascontiguousarray(np.asarray(conv_b, f32)[0].reshape(44, 128).T),
        "ident": np.eye(128, dtype=f32),
        "rmask": rmask,
    }
    in_maps = []
    for c in range(NCORES):
        t0 = c * TPC - (NPRE + 1) * 128
        xe = np.zeros((NX * 128, D), f32)
        tke = np.zeros((NX * 128, 2, 512), f32)
        lo = max(t0, 0)
        xe[lo - t0:] = x[lo:(c + 1) * TPC]
        tke[lo - t0:] = tk[lo:(c + 1) * TPC]
        tqe = np.zeros((NM * 128, 2, 512), f32)
        q0 = c * TPC - 128
        ql = max(q0, 0)
        tqe[ql - q0:] = tq[ql:(c + 1) * TPC]
        am = np.stack([np.tile(cur, (1, 4)), np.tile(prev, (1, 4)), np.tile(prev, (1, 4)) * (1.0 if c > 0 else 0.0)], axis=1).astype(f32)
        m = dict(common)
        m["x_ext"] = xe
        m["amask"] = np.ascontiguousarray(am)
        m["tq"] = tqe
        m["tk"] = tke
        m["coef"] = np.zeros((1, 40), f32)
        in_maps.append(m)
    import os
    if os.environ.get("KSAME"):
        in_maps = [in_maps[int(os.environ["KSAME"])]] * NCORES
    res = run_bass_kernel_spmd(nc, in_maps, core_ids=list(range(NCORES)))
    outs = [np.asarray(r["out"], f32) for r in res.results]
    return np.concatenate(outs, axis=0)[None]
```
